# Optimizing a Trainium2 kernel written in Bass

```python
import math
import jax, jax.numpy as jnp
from jax import lax
import numpy as np

D_MODEL = 1024
BATCH = 8
SEQ = 4096
DEPTH = 2

N_META = 16
S5_GROUP = 16
S5_GROUPS = D_MODEL // S5_GROUP
S5_STATE = 64
N_HEADS = 16
HEAD_DIM = D_MODEL // N_HEADS
D_FF = ((8 * D_MODEL + 3 * 256 - 1) // (3 * 256)) * 256
Q_BLOCK = 128
N_A_LAYERS = DEPTH // 2
N_B_LAYERS = DEPTH - N_A_LAYERS
RMS_EPS = 1e-6
DT_MIN = 1e-3
DT_MAX = 1e-1

kernel_name = "s5_yoco_stickbreaking_hybrid"


def rmsnorm(x, gain):
    x32 = x.astype(jnp.float32)
    y = x32 * lax.rsqrt(jnp.mean(x32 * x32, axis=-1, keepdims=True) + RMS_EPS)
    return (y * gain.astype(jnp.float32)).astype(x.dtype)


def s5_mixer(u, a_re, a_im, log_dt, b_re, b_im, c_re, c_im, d_skip, w_glu):
    bsz, L, _ = u.shape
    f32 = jnp.float32
    u32 = u.astype(f32).reshape(bsz, L, S5_GROUPS, S5_GROUP)
    a_re = a_re.astype(f32)
    a_im = a_im.astype(f32)
    dt = jnp.exp(log_dt.astype(f32))[:, None]
    mag = jnp.exp(dt * a_re)
    ang = dt * a_im
    abar_re = mag * jnp.cos(ang)
    abar_im = mag * jnp.sin(ang)
    den = a_re * a_re + a_im * a_im
    coef_re = ((abar_re - 1.0) * a_re + abar_im * a_im) / den
    coef_im = (abar_im * a_re - (abar_re - 1.0) * a_im) / den
    b_re = b_re.astype(f32)
    b_im = b_im.astype(f32)
    bbar_re = coef_re[..., None] * b_re - coef_im[..., None] * b_im
    bbar_im = coef_re[..., None] * b_im + coef_im[..., None] * b_re
    bu_re = jnp.einsum('blgc,gpc->blgp', u32, bbar_re)
    bu_im = jnp.einsum('blgc,gpc->blgp', u32, bbar_im)
    a_seq_re = jnp.broadcast_to(abar_re, (1, L, S5_GROUPS, S5_STATE))
    a_seq_im = jnp.broadcast_to(abar_im, (1, L, S5_GROUPS, S5_STATE))

    def combine(e1, e2):
        a1r, a1i, b1r, b1i = e1
        a2r, a2i, b2r, b2i = e2
        return (a2r * a1r - a2i * a1i,
                a2r * a1i + a2i * a1r,
                a2r * b1r - a2i * b1i + b2r,
                a2r * b1i + a2i * b1r + b2i)

    _, _, x_re, x_im = lax.associative_scan(combine, (a_seq_re, a_seq_im, bu_re, bu_im), axis=1)
    y = (jnp.einsum('blgp,gcp->blgc', x_re, c_re.astype(f32))
         - jnp.einsum('blgp,gcp->blgc', x_im, c_im.astype(f32)))
    y = (y + d_skip.astype(f32).reshape(S5_GROUPS, S5_GROUP) * u32).reshape(bsz, L, D_MODEL)
    z = jax.nn.gelu(y)
    vg = jnp.einsum('bld,de->ble', z, w_glu.astype(f32))
    val, gate = jnp.split(vg, 2, axis=-1)
    return (val * jax.nn.sigmoid(gate)).astype(u.dtype)


def stick_breaking_attention(q, k, v):
    f32 = jnp.float32
    q = q.astype(f32)
    k = k.astype(f32)
    v = v.astype(f32)
    L = q.shape[1]
    scale = 1.0 / math.sqrt(HEAD_DIM)
    n_real_blocks = (L - N_META) // Q_BLOCK
    bounds = [(0, N_META)] + [(N_META + i * Q_BLOCK, N_META + (i + 1) * Q_BLOCK) for i in range(n_real_blocks)]
    outs = []
    for q0, q1 in bounds:
        qb = q[:, q0:q1]
        kb = k[:, :q1]
        vb = v[:, :q1]
        z = jnp.einsum('bqhd,bkhd->bhqk', qb, kb) * scale
        t_idx = jnp.arange(q0, q1)[:, None]
        s_idx = jnp.arange(q1)[None, :]
        strict = s_idx < t_idx
        log_beta = jax.nn.log_sigmoid(z)
        log_1m = jnp.where(strict, jax.nn.log_sigmoid(-z), 0.0)
        log_remain = lax.cumsum(log_1m, axis=3, reverse=True) - log_1m
        w = jnp.where(strict, jnp.exp(log_beta + log_remain), 0.0)
        outs.append(jnp.einsum('bhqk,bkhd->bqhd', w, vb))
    return jnp.concatenate(outs, axis=1)


def swiglu_ffn(h, w_in, w_out):
    gu = jnp.einsum('bld,df->blf', h, w_in)
    g, u = jnp.split(gu, 2, axis=-1)
    return jnp.einsum('blf,fd->bld', jax.nn.silu(g) * u, w_out)


def setup_inputs(seed: int = 0) -> dict:
    key = jax.random.key(seed)
    ks = jax.random.split(key, 24)
    f32 = jnp.float32
    G, P, C = S5_GROUPS, S5_STATE, S5_GROUP
    HD = N_HEADS * HEAD_DIM
    x = jax.random.normal(ks[0], (BATCH, SEQ, D_MODEL), f32)
    meta_tokens = jax.random.normal(ks[1], (N_META, D_MODEL), f32)
    norm_mix = 1.0 + 0.02 * jax.random.normal(ks[2], (DEPTH, D_MODEL), f32)
    norm_ffn = 1.0 + 0.02 * jax.random.normal(ks[3], (DEPTH, D_MODEL), f32)
    n_idx = jnp.arange(P, dtype=f32)
    s5_a_re = -0.5 + 0.01 * jax.random.normal(ks[4], (N_A_LAYERS, G, P), f32)
    s5_a_im = math.pi * n_idx + 0.01 * jax.random.normal(ks[5], (N_A_LAYERS, G, P), f32)
    s5_log_dt = jax.random.uniform(ks[6], (N_A_LAYERS, G), f32, math.log(DT_MIN), math.log(DT_MAX))
    s5_b_re = jax.random.normal(ks[7], (N_A_LAYERS, G, P, C), f32) * (2 * C) ** -0.5
    s5_b_im = jax.random.normal(ks[8], (N_A_LAYERS, G, P, C), f32) * (2 * C) ** -0.5
    s5_c_re = jax.random.normal(ks[9], (N_A_LAYERS, G, C, P), f32) * P ** -0.5
    s5_c_im = jax.random.normal(ks[10], (N_A_LAYERS, G, C, P), f32) * P ** -0.5
    s5_d = jax.random.normal(ks[11], (N_A_LAYERS, D_MODEL), f32)
    s5_w_glu = jax.random.normal(ks[12], (N_A_LAYERS, D_MODEL, 2 * D_MODEL), f32) * D_MODEL ** -0.5
    norm_kv = 1.0 + 0.02 * jax.random.normal(ks[13], (D_MODEL,), f32)
    w_kv = jax.random.normal(ks[14], (D_MODEL, 2 * HD), f32) * D_MODEL ** -0.5
    w_q = jax.random.normal(ks[15], (N_B_LAYERS, D_MODEL, HD), f32) * D_MODEL ** -0.5
    w_o = jax.random.normal(ks[16], (N_B_LAYERS, HD, D_MODEL), f32) * HD ** -0.5
    w_ffn_in = jax.random.normal(ks[17], (DEPTH, D_MODEL, 2 * D_FF), f32) * D_MODEL ** -0.5
    w_ffn_out = jax.random.normal(ks[18], (DEPTH, D_FF, D_MODEL), f32) * D_FF ** -0.5
    norm_final = 1.0 + 0.02 * jax.random.normal(ks[19], (D_MODEL,), f32)
    return {"x": x, "meta_tokens": meta_tokens, "norm_mix": norm_mix, "norm_ffn": norm_ffn,
            "s5_a_re": s5_a_re, "s5_a_im": s5_a_im, "s5_log_dt": s5_log_dt,
            "s5_b_re": s5_b_re, "s5_b_im": s5_b_im, "s5_c_re": s5_c_re, "s5_c_im": s5_c_im,
            "s5_d": s5_d, "s5_w_glu": s5_w_glu, "norm_kv": norm_kv, "w_kv": w_kv,
            "w_q": w_q, "w_o": w_o, "w_ffn_in": w_ffn_in, "w_ffn_out": w_ffn_out,
            "norm_final": norm_final}


def reference(x, meta_tokens, norm_mix, norm_ffn, s5_a_re, s5_a_im, s5_log_dt,
              s5_b_re, s5_b_im, s5_c_re, s5_c_im, s5_d, s5_w_glu, norm_kv, w_kv,
              w_q, w_o, w_ffn_in, w_ffn_out, norm_final):
    bsz = x.shape[0]
    meta = jnp.broadcast_to(meta_tokens.astype(x.dtype)[None], (bsz, N_META, D_MODEL))
    h = jnp.concatenate([meta, x], axis=1)
    L = h.shape[1]
    k_shared = None
    v_shared = None
    for i in range(DEPTH):
        if i < N_A_LAYERS:
            a = i
            h = h + s5_mixer(rmsnorm(h, norm_mix[i]), s5_a_re[a], s5_a_im[a], s5_log_dt[a],
                             s5_b_re[a], s5_b_im[a], s5_c_re[a], s5_c_im[a], s5_d[a], s5_w_glu[a])
        else:
            j = i - N_A_LAYERS
            q = jnp.einsum('bld,de->ble', rmsnorm(h, norm_mix[i]), w_q[j]).reshape(bsz, L, N_HEADS, HEAD_DIM)
            o = stick_breaking_attention(q, k_shared, v_shared).astype(h.dtype)
            h = h + jnp.einsum('ble,ed->bld', o.reshape(bsz, L, N_HEADS * HEAD_DIM), w_o[j])
        h = h + swiglu_ffn(rmsnorm(h, norm_ffn[i]), w_ffn_in[i], w_ffn_out[i])
        if i == N_A_LAYERS - 1:
            kv = jnp.einsum('bld,de->ble', rmsnorm(h, norm_kv), w_kv).reshape(bsz, L, 2, N_HEADS, HEAD_DIM)
            k_shared = kv[:, :, 0]
            v_shared = kv[:, :, 1]
    out = rmsnorm(h, norm_final)
    return out[:, N_META:]
```

```python
import math
import numpy as np
import concourse.bass as bass
import concourse.mybir as mybir
from concourse.bass_utils import run_bass_kernel_spmd

F32 = mybir.dt.float32
BF16 = mybir.dt.bfloat16
I32 = mybir.dt.int32
AF = mybir.ActivationFunctionType
ALU = mybir.AluOpType

D = 1024
FC = 8
SEQ = 4096
NMETA = 16
L = SEQ + NMETA
G = 64
DFF = 2816
MF = DFF // 128
NH = 16
EPS = 1e-6
ENGS = ("pe", "act", "dve", "pool", "sp")
EPOCH = 12000


class Op:
    __slots__ = ("eng", "fn", "deps", "sig", "seq", "key", "kval", "is_dma")

    def __init__(self, eng, fn, deps, is_dma=False, key=None):
        self.eng = eng
        self.fn = fn
        self.deps = [d for d in deps if d is not None]
        self.sig = False
        self.seq = None
        self.key = key
        self.kval = None
        self.is_dma = is_dma


class Sched:
    def __init__(self, nc):
        self.nc = nc
        self.ops = {e: [] for e in ENGS}
        self.keycount = {}

    def add(self, eng, fn, deps=()):
        op = Op(eng, fn, deps)
        self.ops[eng].append(op)
        return op

    def dma(self, queue, fn, key, deps=()):
        op = Op(queue, fn, deps, is_dma=True, key=key)
        self.keycount[key] = self.keycount.get(key, 0) + 1
        op.kval = 16 * self.keycount[key]
        self.ops[queue].append(op)
        return op

    def emit(self):
        nc = self.nc
        for e in ENGS:
            for op in self.ops[e]:
                for d in op.deps:
                    if d.is_dma:
                        continue
                    if d.eng == "pe" and op.eng == "pe" and not op.is_dma:
                        continue
                    d.sig = True
        nsig = {}
        for e in ENGS:
            n = 0
            for op in self.ops[e]:
                if op.sig and not op.is_dma:
                    op.seq = n
                    n += 1
            nsig[e] = n
        sems = {}
        for e in ENGS:
            for ep in range((nsig[e] + EPOCH - 1) // EPOCH):
                sems[(e, ep)] = nc.alloc_semaphore(f"c_{e}_{ep}")
        ksems = {k: nc.alloc_semaphore(f"k_{k}") for k in self.keycount}
        engobj = {"pe": nc.tensor, "act": nc.scalar, "dve": nc.vector, "pool": nc.gpsimd, "sp": nc.sync}

        def run_engine(e):
            eng = engobj[e]
            waited = {}
            for op in self.ops[e]:
                need = {}
                for d in op.deps:
                    if d.is_dma:
                        s = ksems[d.key]
                        v = d.kval
                    else:
                        if d.eng == "pe" and e == "pe" and not op.is_dma:
                            continue
                        s = sems[(d.eng, d.seq // EPOCH)]
                        v = d.seq % EPOCH + 1
                    if need.get(s.num, (None, 0))[1] < v:
                        need[s.num] = (s, v)
                for num, (s, v) in need.items():
                    if waited.get(num, 0) < v:
                        eng.wait_ge(s, v)
                        waited[num] = v
                ins = op.fn(eng)
                if op.is_dma:
                    ins.then_inc(ksems[op.key], 16)
                elif op.sig:
                    ins.then_inc(sems[(e, op.seq // EPOCH)], 1)

        with nc.Block() as block:
            @block.tensor
            def _(eng):
                run_engine("pe")

            @block.scalar
            def _(eng):
                run_engine("act")

            @block.vector
            def _(eng):
                run_engine("dve")

            @block.gpsimd
            def _(eng):
                run_engine("pool")

            @block.sync
            def _(eng):
                run_engine("sp")


class Buf:
    def __init__(self):
        self.w = {}
        self.r = {}
        self.prev = []

    @staticmethod
    def _put(d, op):
        d[id(op) if op.is_dma else op.eng] = op

    def rd(self):
        return list(self.w.values())

    def wr(self):
        return list(self.w.values()) + list(self.r.values())

    def did_read(self, op):
        self._put(self.r, op)

    def did_write(self, op):
        self.w = {}
        self.r = {}
        self._put(self.w, op)

    def start_gen(self):
        self.prev = self.wr()
        self.w = {}
        self.r = {}
        return self.prev

    def also_wrote(self, op):
        self._put(self.w, op)


def build_nc(dbg=False, upto="all"):
    nc = bass.Bass("TRN2", target_bir_lowering=False)
    S = Sched(nc)

    def din(name, shape):
        return nc.dram_tensor(name, list(shape), F32, kind="ExternalInput")

    x_d = din("x", (SEQ, D))
    meta_d = din("meta_tokens", (NMETA, D))
    nmix_d = din("norm_mix", (2, D))
    nffn_d = din("norm_ffn", (2, D))
    are_d = din("s5_a_re", (G, 64))
    aim_d = din("s5_a_im", (G, 64))
    ldt_d = din("s5_log_dt", (1, G))
    bre_d = din("s5_b_re", (G, 64, 16))
    bim_d = din("s5_b_im", (G, 64, 16))
    cre_d = din("s5_c_re", (G * 16, 64))
    cim_d = din("s5_c_im", (G * 16, 64))
    sd_d = din("s5_d", (1, D))
    wglu_d = din("s5_w_glu", (D, 2 * D))
    nkv_d = din("norm_kv", (1, D))
    wkv_d = din("w_kv", (D, 2 * D))
    wq_d = din("w_q", (D, D))
    wo_d = din("w_o", (D, D))
    win_d = din("w_ffn_in", (2, D, 2 * DFF))
    wout_d = din("w_ffn_out", (2, DFF, D))
    nfin_d = din("norm_final", (1, D))
    out_d = nc.dram_tensor("out", [SEQ, D], F32, kind="ExternalOutput")
    dbg_d = nc.dram_tensor("dbg", [L, D], F32, kind="ExternalOutput") if dbg else None
    dbgW = nc.dram_tensor("dbgW", [128, 2 * G * 64], F32, kind="ExternalOutput") if dbg else None
    dbgW0 = nc.dram_tensor("dbgW0", [128, 2 * G * 64], F32, kind="ExternalOutput") if dbg else None
    dbgU = nc.dram_tensor("dbgU", [128, G * 128], BF16, kind="ExternalOutput") if dbg else None
    dbgcnt = [0]

    KT_all = nc.dram_tensor("kt_all", [D, L], BF16, kind="Internal")
    V_all = nc.dram_tensor("v_all", [L, D], BF16, kind="Internal")
    s5scr = [nc.dram_tensor(f"s5scr{i}", [128, G * 128], BF16, kind="Internal") for i in range(4)]

    SB0 = 16512
    SBEND = 229344
    cur = [SB0]

    def alloc(name, shape, dt, at=None):
        nbytes = int(np.prod(shape[1:])) * (4 if dt in (F32, I32) else 2)
        nbytes = (nbytes + 31) // 32 * 32
        if at is None:
            off = cur[0]
            cur[0] += nbytes
            assert cur[0] <= SBEND, (name, cur[0])
        else:
            off = at
            assert off + nbytes <= SBEND, (name, off + nbytes)
        return nc.alloc_sbuf_tensor_at(name, list(shape), dt, offset=off), off + nbytes

    ident, _ = alloc("ident", [128, 128], BF16)
    identf, _ = alloc("identf", [128, 128], F32)
    onesf, _ = alloc("onesf", [128, 128], F32)
    TU, _ = alloc("TU", [128, 128], BF16)
    TL, _ = alloc("TL", [128, 128], BF16)
    dmask, _ = alloc("dmask", [128, 4, 512], BF16)
    gcol, _ = alloc("gcol", [128, 5, 8], F32)
    gfin, _ = alloc("gfin", [128, D], F32)
    gcolS5, _ = alloc("gcolS5", [128, G], F32)
    A1, _ = alloc("A1", [128, 2, G], F32)
    A2, _ = alloc("A2", [128, 2, G], F32)
    Zc, _ = alloc("Zc", [128, 2, G], F32)
    sT1, _ = alloc("sT1", [128, 2, G], F32)
    sT2, _ = alloc("sT2", [128, 2, G], F32)
    ssq, _ = alloc("ssq", [128, 8], F32)
    rstd, _ = alloc("rstd", [128, 8], F32)
    ring, _ = alloc("ring", [128, 4, 4096], BF16)
    h, _ = alloc("h", [128, 8, D], F32)
    ARENA = cur[0] - 32768
    hs, _ = alloc("hs", [128, 8 * D], BF16)
    xnT, _ = alloc("xnT", [128, FC, 1024], BF16)
    OV = cur[0]

    U, e1 = alloc("U", [128, G, 128], BF16, at=OV)
    Wbuf, e2 = alloc("Wbuf", [128, 64, 2, G], F32, at=e1)
    Xb, e3 = alloc("Xb", [128, G, 128], BF16, at=e2)
    sgb, _ = alloc("sgb", [128, 8, 512], F32, at=e1)
    outs, _ = alloc("outs", [128, 8, D], F32, at=OV)
    act, f1 = alloc("act", [128, MF, 1024], BF16, at=OV)
    woutb, f2 = alloc("woutb", [128, MF, 1024], BF16, at=f1)
    silu_t, f3 = alloc("silu_t", [128, 2, 512], F32, at=f2)
    Kst, k1 = alloc("Kst", [128, 8, 1024], BF16, at=OV)
    Vst, k2 = alloc("Vst", [128, 8, 1024], BF16, at=k1)
    QT, a1 = alloc("QT", [128, 8, 1024], BF16, at=OV)
    oT, a2 = alloc("oT", [128, 8, 1024], BF16, at=a1)
    KTp, a3 = alloc("KTp", [128, 2, 4128], BF16, at=a2)
    Vp, a4 = alloc("Vp", [128, 2, 33 * 128], BF16, at=a3)
    xnT2, _ = alloc("xnT2", [128, FC, 1024], BF16, at=a4)
    Et, a5 = alloc("Et", [128, 2, 1024], F32, at=a4)
    SPt, a6 = alloc("SPt", [128, 2, 1024], F32, at=a5)
    ARGt, a7 = alloc("ARGt", [128, 2, 1024], F32, at=a6)
    LMt, a8 = alloc("LMt", [128, 2, 1024], BF16, at=a7)
    Wtt, a9 = alloc("Wtt", [128, 2, 1024], BF16, at=a8)

    PBt = nc.alloc_psum_tensor("pball", [128, 8 * 512], F32)

    class Bank:
        def __init__(self, idx):
            self.idx = idx

        def __getitem__(self, key):
            if not isinstance(key, tuple):
                key = (key, slice(0, 512))
            rows, cols = key
            c0 = 0 if cols.start is None else cols.start
            c1 = 512 if cols.stop is None else cols.stop
            return PBt[rows, 512 * self.idx + c0:512 * self.idx + c1]

        def ap(self, off, dims, parts=128):
            return bass.AP(PBt, 512 * self.idx + off, [[4096, parts]] + [list(d) for d in dims])

    PB = [Bank(i) for i in range(8)]
    pbuf = [Buf() for _ in range(8)]
    rot = [0]

    rotmod = [4]

    def next_bank():
        b = rot[0] % rotmod[0]
        rot[0] = (b + 1) % rotmod[0]
        return b

    def sap(t, off, dims, parts=128, p0=0):
        pstep = int(np.prod(t.shape[1:]))
        return bass.AP(t, p0 * pstep + off, [[pstep, parts]] + [list(d) for d in dims])

    B = {n: Buf() for n in ["h", "hs", "xnT", "ssq", "rstd", "U", "Wbuf", "Xb", "Zc", "sT1", "sT2", "sgb", "act", "woutb",
                            "Kst", "Vst", "QT", "oT", "const", "silu0", "silu1", "xnT2", "outs"]}
    ringB = [Buf() for _ in range(4)]
    ringi = [0]

    def OP(eng, fn, reads=(), writes=(), partial=(), extra=()):
        deps = list(extra)
        for b in reads:
            deps += b.rd()
        for b in writes:
            deps += b.wr()
        for b in partial:
            deps += b.prev
        op = S.add(eng, fn, deps)
        for b in reads:
            b.did_read(op)
        for b in writes:
            b.did_write(op)
        for b in partial:
            b.also_wrote(op)
        return op

    def DMA(queue, fn, key, reads=(), writes=(), partial=(), extra=()):
        deps = list(extra)
        for b in reads:
            deps += b.rd()
        for b in writes:
            deps += b.wr()
        for b in partial:
            deps += b.prev
        op = S.dma(queue, fn, key, deps)
        for b in reads:
            b.did_read(op)
        for b in writes:
            b.did_write(op)
        for b in partial:
            b.also_wrote(op)
        return op

    CONST = B["const"]

    OP("pool", lambda e: e.memset(onesf[:], 1.0), writes=[Buf()])
    o_ones = S.ops["pool"][-1]
    c_ops = []
    c_ops.append(S.add("pool", lambda e: e.affine_select(out=identf[:], in_=onesf[:], pattern=[[-1, 128]], compare_op=ALU.is_equal,
                                                         fill=0.0, base=0, channel_multiplier=1), [o_ones]))
    c_ops.append(S.add("pool", lambda e: e.affine_select(out=ident[:], in_=onesf[:], pattern=[[-1, 128]], compare_op=ALU.is_equal,
                                                         fill=0.0, base=0, channel_multiplier=1), [o_ones]))
    c_ops.append(S.add("pool", lambda e: e.affine_select(out=TU[:], in_=onesf[:], pattern=[[-1, 128]], compare_op=ALU.is_gt,
                                                         fill=0.0, base=0, channel_multiplier=1), [o_ones]))
    c_ops.append(S.add("pool", lambda e: e.affine_select(out=TL[:], in_=onesf[:], pattern=[[1, 128]], compare_op=ALU.is_ge,
                                                         fill=0.0, base=0, channel_multiplier=-1), [o_ones]))
    for d in range(4):
        c_ops.append(S.add("pool", lambda e, d=d: e.affine_select(out=dmask[:, d, :], in_=sap(onesf, 0, [[0, 4], [1, 128]]),
                                                                  pattern=[[1, 512]], compare_op=ALU.is_gt, fill=0.0,
                                                                  base=(16 if d == 1 else -128 * d), channel_multiplier=-1), [o_ones]))
    with nc.allow_non_contiguous_dma(reason="tiny one-time parameter layouts"):
        pass
    gsrcs = [(nmix_d, 0), (nffn_d, 0), (nkv_d, 0), (nmix_d, 1), (nffn_d, 1)]

    def ncdma(e, out, in_):
        return e.dma_start(out=out, in_=in_, allow_slow_non_contiguous=True)

    for i, (src, row) in enumerate(gsrcs):
        c_ops.append(S.dma("act", lambda e, i=i, src=src, row=row: ncdma(
            e, gcol[:, i, :], bass.AP(src, row * D, [[1, 128], [128, 8]])), "cst"))
    c_ops.append(S.dma("act", lambda e: e.dma_start(out=gfin[:], in_=bass.AP(nfin_d, 0, [[0, 128], [1, D]])), "cst"))
    for j in range(8):
        c_ops.append(S.dma("act", lambda e, j=j: ncdma(e, gcolS5[16 * j:16 * j + 16, :], bass.AP(nmix_d, 0, [[1, 16], [16, G]])), "cst"))
    for op in c_ops:
        CONST.also_wrote(op)

    scur = [ARENA]

    def salloc(name, shape, dt):
        t, e = alloc(name, shape, dt, at=scur[0])
        scur[0] = e
        return t

    ARE = salloc("ARE", [128, G], F32)
    AIM = salloc("AIM", [128, G], F32)
    DT = salloc("DT", [128, G], F32)
    X1 = salloc("X1", [128, G], F32)
    ANG = salloc("ANG", [128, G], F32)
    TMPA = salloc("TMPA", [128, G], F32)
    TMPB = salloc("TMPB", [128, G], F32)
    TMPI = salloc("TMPI", [128, G], I32)
    MAG = salloc("MAG", [128, G], F32)
    IMG2 = salloc("IMG2", [128, G], F32)
    COSA = salloc("COSA", [128, G], F32)
    SINA = salloc("SINA", [128, G], F32)
    PWR = salloc("PWR", [128, 16, G], F32)
    PWI = salloc("PWI", [128, 16, G], F32)
    CFR = salloc("CFR", [128, G], F32)
    CFI = salloc("CFI", [128, G], F32)
    SGN = salloc("SGN", [128, 1], F32)
    DCOL = salloc("DCOL", [128, G], F32)
    BRE = salloc("BRE", [128, G, 16], F32)
    BIM = salloc("BIM", [128, G, 16], F32)
    BB1 = salloc("BB1", [128, G, 16], F32)
    BB2 = salloc("BB2", [128, G, 16], F32)
    CC1 = salloc("CC1", [128, G, 16], F32)
    CC2 = salloc("CC2", [128, G, 16], F32)
    CNR = salloc("CNR", [128, 8, 128], F32)
    CNI = salloc("CNI", [128, 8, 128], F32)
    TM1 = salloc("TM1", [128, G * 16], F32)
    TM2 = salloc("TM2", [128, G * 16], F32)
    BLK = salloc("BLK", [128, 128], F32)
    TMM = salloc("TMM", [128, 512], F32)
    PA = salloc("PA", [128, G, 128], BF16)
    QAm = salloc("QAm", [128, G, 128], BF16)
    QAo = salloc("QAo", [128, G, 128], BF16)
    Mm = salloc("Mm", [128, G, 128], BF16)
    P1 = salloc("P1", [128, G, 128], BF16)
    P2 = salloc("P2", [128, G, 128], BF16)

    sb = {}

    def T(name):
        if name not in sb:
            sb[name] = Buf()
        return sb[name]

    def dve(fn, reads, writes):
        return OP("dve", fn, reads=[T(r) for r in reads], writes=[T(w) for w in writes])

    def actop(fn, reads, writes):
        return OP("act", fn, reads=[T(r) for r in reads], writes=[T(w) for w in writes])

    for half in range(2):
        ps_ = slice(64 * half, 64 * half + 64)
        DMA("sp", lambda e, ps_=ps_: ncdma(e, ARE[ps_, :], bass.AP(are_d, 0, [[1, 64], [64, G]])), "s5l_ARE", partial=[T("ARE")])
        DMA("sp", lambda e, ps_=ps_: ncdma(e, AIM[ps_, :], bass.AP(aim_d, 0, [[1, 64], [64, G]])), "s5l_AIM", partial=[T("AIM")])
        DMA("sp", lambda e, ps_=ps_: ncdma(e, BRE[ps_, :, :], bass.AP(bre_d, 0, [[16, 64], [1024, G], [1, 16]])), "s5l_BRE", partial=[T("BRE")])
        DMA("sp", lambda e, ps_=ps_: ncdma(e, BIM[ps_, :, :], bass.AP(bim_d, 0, [[16, 64], [1024, G], [1, 16]])), "s5l_BIM", partial=[T("BIM")])
        DMA("sp", lambda e, half=half: ncdma(e, CNR[:, :, 64 * half:64 * half + 64], bass.AP(cre_d, 0, [[64, 128], [8192, 8], [1, 64]])),
            "s5l_CNR", partial=[T("CNR")])
        DMA("sp", lambda e, half=half: ncdma(e, CNI[:, :, 64 * half:64 * half + 64], bass.AP(cim_d, 0, [[64, 128], [8192, 8], [1, 64]])),
            "s5l_CNI", partial=[T("CNI")])
    DMA("sp", lambda e: e.dma_start(out=DT[:], in_=bass.AP(ldt_d, 0, [[0, 128], [1, G]])), "s5l_DT", writes=[T("DT")])
    for j in range(8):
        DMA("sp", lambda e, j=j: ncdma(e, DCOL[16 * j:16 * j + 16, :], bass.AP(sd_d, 0, [[1, 16], [16, G]])), "s5l_DCOL", partial=[T("DCOL")])
    OP("pool", lambda e: e.memset(SGN[0:64, :], -1.0), partial=[T("SGN")])
    OP("pool", lambda e: e.memset(SGN[64:128, :], 1.0), partial=[T("SGN")])
    OP("pool", lambda e: e.memset(Zc[:], 0.0), writes=[B["Zc"]])
    for i in range(8):
        OP("pool", lambda e, i=i: e.affine_select(out=BLK[:, 16 * i:16 * i + 16], in_=onesf[:, 0:16], pattern=[[0, 16]], compare_op=ALU.is_gt,
                                                  fill=0.0, base=16 * (i + 1), channel_multiplier=-1), reads=[CONST], partial=[T("BLK")])

    actop(lambda e: e.activation(out=DT[:], in_=DT[:], func=AF.Exp), ["DT"], ["DT"])
    dve(lambda e: e.tensor_tensor(out=X1[:], in0=DT[:], in1=ARE[:], op=ALU.mult), ["DT", "ARE"], ["X1"])
    dve(lambda e: e.tensor_tensor(out=ANG[:], in0=DT[:], in1=AIM[:], op=ALU.mult), ["DT", "AIM"], ["ANG"])
    dve(lambda e: e.tensor_scalar(out=TMPA[:], in0=X1[:], scalar1=1.0 / 720.0, scalar2=None, op0=ALU.mult), ["X1"], ["TMPA"])
    for cst in (1.0 / 120.0, 1.0 / 24.0, 1.0 / 6.0, 0.5, 1.0):
        dve(lambda e, cst=cst: e.scalar_tensor_tensor(out=TMPA[:], in0=TMPA[:], scalar=cst, in1=X1[:], op0=ALU.add, op1=ALU.mult),
            ["TMPA", "X1"], ["TMPA"])
    dve(lambda e: e.tensor_scalar(out=MAG[:], in0=TMPA[:], scalar1=1.0, scalar2=None, op0=ALU.add), ["TMPA"], ["MAG"])
    dve(lambda e: e.tensor_scalar(out=TMPB[:], in0=X1[:], scalar1=-2.0, scalar2=None, op0=ALU.mult), ["X1"], ["TMPB"])
    dve(lambda e: e.tensor_scalar(out=TMPA[:], in0=TMPB[:], scalar1=1.0 / 720.0, scalar2=None, op0=ALU.mult), ["TMPB"], ["TMPA"])
    for cst in (1.0 / 120.0, 1.0 / 24.0, 1.0 / 6.0, 0.5, 1.0):
        dve(lambda e, cst=cst: e.scalar_tensor_tensor(out=TMPA[:], in0=TMPA[:], scalar=cst, in1=TMPB[:], op0=ALU.add, op1=ALU.mult),
            ["TMPA", "TMPB"], ["TMPA"])
    dve(lambda e: e.tensor_scalar(out=IMG2[:], in0=TMPA[:], scalar1=1.0, scalar2=None, op0=ALU.add), ["TMPA"], ["IMG2"])

    def trig(dst, dname, shift):
        dve(lambda e: e.tensor_scalar(out=TMPA[:], in0=ANG[:], scalar1=shift, scalar2=1.0 / (2 * math.pi), op0=ALU.add, op1=ALU.mult),
            ["ANG"], ["TMPA"])
        dve(lambda e: e.tensor_copy(out=TMPI[:], in_=TMPA[:]), ["TMPA"], ["TMPI"])
        dve(lambda e: e.tensor_copy(out=TMPB[:], in_=TMPI[:]), ["TMPI"], ["TMPB"])
        dve(lambda e: e.scalar_tensor_tensor(out=TMPA[:], in0=TMPB[:], scalar=-2 * math.pi, in1=ANG[:], op0=ALU.mult, op1=ALU.add),
            ["TMPB", "ANG"], ["TMPA"])
        dve(lambda e: e.tensor_scalar(out=TMPA[:], in0=TMPA[:], scalar1=shift, scalar2=math.pi, op0=ALU.add, op1=ALU.min),
            ["TMPA"], ["TMPA"])
        dve(lambda e: e.tensor_scalar(out=TMPA[:], in0=TMPA[:], scalar1=-math.pi, scalar2=None, op0=ALU.max), ["TMPA"], ["TMPA"])
        actop(lambda e: e.activation(out=dst[:], in_=TMPA[:], func=AF.Sin), ["TMPA"], [dname])

    sb["SINA"] = Buf()
    sb["COSA"] = Buf()
    trig(SINA, 'SINA', 0.0)
    trig(COSA, 'COSA', math.pi / 2)
    T7 = 7
    OP("pool", lambda e: e.memset(PWR[:, T7, :], 1.0), partial=[T("PWR")])
    OP("pool", lambda e: e.memset(PWI[:, T7, :], 0.0), partial=[T("PWI")])
    dve(lambda e: e.tensor_tensor(out=PWR[:, T7 + 1, :], in0=MAG[:], in1=COSA[:], op=ALU.mult), ["MAG", "COSA"], [])
    o = S.ops["dve"][-1]
    T("PWR").also_wrote(o)
    dve(lambda e: e.tensor_tensor(out=PWI[:, T7 + 1, :], in0=MAG[:], in1=SINA[:], op=ALU.mult), ["MAG", "SINA"], [])
    o = S.ops["dve"][-1]
    T("PWI").also_wrote(o)
    dve(lambda e: e.tensor_tensor(out=PWR[:, T7 - 1, :], in0=PWR[:, T7 + 1, :], in1=IMG2[:], op=ALU.mult), ["PWR", "IMG2"], [])
    T("PWR").also_wrote(S.ops["dve"][-1])
    dve(lambda e: e.scalar_tensor_tensor(out=PWI[:, T7 - 1, :], in0=PWI[:, T7 + 1, :], scalar=-1.0, in1=IMG2[:], op0=ALU.mult, op1=ALU.mult),
        ["PWI", "IMG2"], [])
    T("PWI").also_wrote(S.ops["dve"][-1])

    def cmul(dst, a, b):
        dve(lambda e: e.tensor_tensor(out=TMPA[:], in0=PWR[:, a, :], in1=PWR[:, b, :], op=ALU.mult), ["PWR"], ["TMPA"])
        dve(lambda e: e.tensor_tensor(out=TMPB[:], in0=PWI[:, a, :], in1=PWI[:, b, :], op=ALU.mult), ["PWI"], ["TMPB"])
        dve(lambda e: e.tensor_tensor(out=PWR[:, dst, :], in0=TMPA[:], in1=TMPB[:], op=ALU.subtract), ["TMPA", "TMPB", "PWR"], [])
        o1 = S.ops["dve"][-1]
        dve(lambda e: e.tensor_tensor(out=TMPA[:], in0=PWR[:, a, :], in1=PWI[:, b, :], op=ALU.mult), ["PWR", "PWI"], ["TMPA"])
        dve(lambda e: e.tensor_tensor(out=TMPB[:], in0=PWI[:, a, :], in1=PWR[:, b, :], op=ALU.mult), ["PWR", "PWI"], ["TMPB"])
        dve(lambda e: e.tensor_tensor(out=PWI[:, dst, :], in0=TMPA[:], in1=TMPB[:], op=ALU.add), ["TMPA", "TMPB", "PWI"], [])
        o2 = S.ops["dve"][-1]
        T("PWR").also_wrote(o1)
        T("PWI").also_wrote(o2)

    for tau in range(2, 9):
        cmul(T7 + tau, T7 + tau - 1, T7 + 1)
    for tau in range(2, 8):
        cmul(T7 - tau, T7 - tau + 1, T7 - 1)

    dve(lambda e: e.tensor_tensor(out=TMPA[:], in0=ARE[:], in1=ARE[:], op=ALU.mult), ["ARE"], ["TMPA"])
    dve(lambda e: e.tensor_tensor(out=TMPB[:], in0=AIM[:], in1=AIM[:], op=ALU.mult), ["AIM"], ["TMPB"])
    dve(lambda e: e.tensor_tensor(out=TMPA[:], in0=TMPA[:], in1=TMPB[:], op=ALU.add), ["TMPA", "TMPB"], ["TMPA"])
    dve(lambda e: e.reciprocal(out=IMG2[:], in_=TMPA[:]), ["TMPA"], ["IMG2"])
    dve(lambda e: e.tensor_scalar(out=MAG[:], in0=PWR[:, T7 + 1, :], scalar1=-1.0, scalar2=None, op0=ALU.add), ["PWR"], ["MAG"])
    dve(lambda e: e.tensor_tensor(out=TMPA[:], in0=MAG[:], in1=ARE[:], op=ALU.mult), ["MAG", "ARE"], ["TMPA"])
    dve(lambda e: e.tensor_tensor(out=TMPB[:], in0=PWI[:, T7 + 1, :], in1=AIM[:], op=ALU.mult), ["PWI", "AIM"], ["TMPB"])
    dve(lambda e: e.tensor_tensor(out=TMPA[:], in0=TMPA[:], in1=TMPB[:], op=ALU.add), ["TMPA", "TMPB"], ["TMPA"])
    dve(lambda e: e.tensor_tensor(out=CFR[:], in0=TMPA[:], in1=IMG2[:], op=ALU.mult), ["TMPA", "IMG2"], ["CFR"])
    dve(lambda e: e.tensor_tensor(out=TMPA[:], in0=PWI[:, T7 + 1, :], in1=ARE[:], op=ALU.mult), ["PWI", "ARE"], ["TMPA"])
    dve(lambda e: e.tensor_tensor(out=TMPB[:], in0=MAG[:], in1=AIM[:], op=ALU.mult), ["MAG", "AIM"], ["TMPB"])
    dve(lambda e: e.tensor_tensor(out=TMPA[:], in0=TMPA[:], in1=TMPB[:], op=ALU.subtract), ["TMPA", "TMPB"], ["TMPA"])
    dve(lambda e: e.tensor_tensor(out=CFI[:], in0=TMPA[:], in1=IMG2[:], op=ALU.mult), ["TMPA", "IMG2"], ["CFI"])

    def bc16(t, idx=None):
        if idx is None:
            return sap(t, 0, [[1, G], [0, 16]])
        return sap(t, idx * G, [[1, G], [0, 16]])

    def halves(fn_top, fn_bot):
        fn_top(slice(0, 64))
        fn_bot(slice(64, 128))

    TM1v = TM1[:].rearrange("p (g c) -> p g c", c=16)
    TM2v = TM2[:].rearrange("p (g c) -> p g c", c=16)
    dve(lambda e: e.tensor_tensor(out=TM1v, in0=BRE[:], in1=bc16(CFR), op=ALU.mult), ["BRE", "CFR"], ["TM1"])
    dve(lambda e: e.tensor_tensor(out=TM2v, in0=BIM[:], in1=bc16(CFI), op=ALU.mult), ["BIM", "CFI"], ["TM2"])
    dve(lambda e: e.tensor_tensor(out=TM1v, in0=TM1v, in1=TM2v, op=ALU.subtract), ["TM1", "TM2"], ["TM1"])
    dve(lambda e: e.tensor_copy(out=BB1[0:64], in_=TM1v[0:64]), ["TM1"], [])
    T("BB1").also_wrote(S.ops["dve"][-1])
    dve(lambda e: e.tensor_copy(out=BB2[64:128], in_=TM1v[64:128]), ["TM1"], [])
    T("BB2").also_wrote(S.ops["dve"][-1])
    dve(lambda e: e.tensor_tensor(out=TM1v, in0=BIM[:], in1=bc16(CFR), op=ALU.mult), ["BIM", "CFR", "TM1"], ["TM1"])
    dve(lambda e: e.tensor_tensor(out=TM2v, in0=BRE[:], in1=bc16(CFI), op=ALU.mult), ["BRE", "CFI"], ["TM2"])
    dve(lambda e: e.tensor_tensor(out=TM1v, in0=TM1v, in1=TM2v, op=ALU.add), ["TM1", "TM2"], ["TM1"])
    dve(lambda e: e.tensor_copy(out=BB1[64:128], in_=TM1v[64:128]), ["TM1"], [])
    T("BB1").also_wrote(S.ops["dve"][-1])
    dve(lambda e: e.tensor_scalar(out=BB2[0:64], in0=TM1v[0:64], scalar1=-1.0, scalar2=None, op0=ALU.mult), ["TM1"], [])
    T("BB2").also_wrote(S.ops["dve"][-1])

    for which, CN, cname in ((0, CNR, 'CNR'), (1, CNI, 'CNI')):
        for blk in range(8):
            b = next_bank()
            pbuf[b].start_gen()
            OP("pe", lambda e, b=b, CN=CN, blk=blk: e.matmul(PB[b][:, 0:128], lhsT=CN[:, blk, :], rhs=identf[:], start=True, stop=True),
               reads=[T(cname), CONST], partial=[pbuf[b]])
            gsl = slice(8 * blk, 8 * blk + 8)
            pv = PB[b][:, 0:128].rearrange("p (g c) -> p g c", c=16)
            if which == 0:
                o_ = OP("act", lambda e, pv=pv, gsl=gsl: e.activation(out=CC1[0:64, gsl, :], in_=pv[0:64], func=AF.Copy),
                        reads=[pbuf[b]], partial=[T("CC1")])
                OP("dve", lambda e, pv=pv, gsl=gsl: e.tensor_scalar(out=CC2[64:128, gsl, :], in0=pv[64:128], scalar1=-1.0, scalar2=None, op0=ALU.mult),
                   reads=[pbuf[b]], partial=[T("CC2")], extra=[o_])
            else:
                o_ = OP("act", lambda e, pv=pv, gsl=gsl: e.activation(out=CC1[64:128, gsl, :], in_=pv[64:128], func=AF.Copy, scale=-1.0),
                        reads=[pbuf[b]], partial=[T("CC1")])
                OP("dve", lambda e, pv=pv, gsl=gsl: e.tensor_scalar(out=CC2[0:64, gsl, :], in0=pv[0:64], scalar1=-1.0, scalar2=None, op0=ALU.mult),
                   reads=[pbuf[b]], partial=[T("CC2")], extra=[o_])

    def combo(dst, dname, slot, tidx, X1t, X2t, x1n, x2n):
        dv = sap(dst, slot * 16, [[128, G], [1, 16]])
        dve(lambda e: e.tensor_tensor(out=TM1v, in0=X1t[:], in1=bc16(PWR, tidx), op=ALU.mult), [x1n, "PWR", "TM1"], ["TM1"])
        OP("pool", lambda e: e.tensor_tensor(out=TM2v, in0=X2t[:], in1=bc16(PWI, tidx), op=ALU.mult),
           reads=[T(x2n), T("PWI")], writes=[T("TM2")])
        dve(lambda e: e.tensor_tensor(out=dv, in0=TM1v, in1=TM2v, op=ALU.add), ["TM1", "TM2"], [])
        T(dname).also_wrote(S.ops["dve"][-1])

    for j in range(8):
        combo(PA, 'PA', j, T7 + (7 - j), BB1, BB2, "BB1", "BB2")
    for i in range(8):
        combo(QAm, 'QAm', i, T7 + (i - 7), CC1, CC2, "CC1", "CC2")
    for i in range(8):
        combo(QAo, 'QAo', i, T7 + (i + 1), CC1, CC2, "CC1", "CC2")

    for g0 in range(0, G, 4):
        b = next_bank()
        pbuf[b].start_gen()
        for gg in range(4):
            OP("pe", lambda e, b=b, g=g0 + gg, gg=gg: e.matmul(PB[b][:, 128 * gg:128 * gg + 128], lhsT=PA[:, g, :], rhs=QAm[:, g, :],
                                                             start=True, stop=True), reads=[T("PA"), T("QAm")], partial=[pbuf[b]])
        OP("dve", lambda e, b=b: e.tensor_tensor(out=TMM[:].rearrange("p (g n) -> p g n", g=4), in0=PB[b][:].rearrange("p (g n) -> p g n", g=4),
                                                in1=sap(BLK, 0, [[0, 4], [1, 128]]), op=ALU.mult),
           reads=[pbuf[b], T("BLK")], writes=[T("TMM")])
        for gg in range(4):
            OP("dve", lambda e, g=g0 + gg, gg=gg: e.scalar_tensor_tensor(out=Mm[:, g, :], in0=identf[:], scalar=DCOL[:, g:g + 1],
                                                                       in1=TMM[:, 128 * gg:128 * gg + 128], op0=ALU.mult, op1=ALU.add),
               reads=[T("TMM"), T("DCOL"), CONST], partial=[T("Mm")])
        b2 = next_bank()
        pbuf[b2].start_gen()
        for gg in range(4):
            OP("pe", lambda e, b2=b2, g=g0 + gg, gg=gg: e.matmul(PB[b2][:, 128 * gg:128 * gg + 128], lhsT=PA[:, g, :], rhs=ident[:],
                                                               start=True, stop=True), reads=[T("PA"), CONST], partial=[pbuf[b2]])
        pv = PB[b2][:].rearrange("p (g n) -> p g n", g=4)
        o1_ = OP("act", lambda e, pv=pv, g0=g0: e.activation(out=P1[:, g0:g0 + 4, :], in_=pv, func=AF.Copy), reads=[pbuf[b2]], partial=[T("P1")])
        o2_ = OP("act", lambda e, pv=pv, g0=g0: e.activation(out=P2[:, g0:g0 + 4, 64:128], in_=pv[:, :, 0:64], func=AF.Copy), reads=[pbuf[b2]],
                 partial=[T("P2")])
        OP("dve", lambda e, pv=pv, g0=g0: e.tensor_copy(out=P2[:, g0:g0 + 4, 0:64], in_=pv[:, :, 64:128]), reads=[pbuf[b2]], partial=[T("P2")],
           extra=[o1_, o2_])
    T8 = T7 + 8
    OP("dve", lambda e: e.tensor_copy(out=A1[:, 0, :], in_=PWR[:, T8, :]), reads=[T("PWR")], partial=[CONST])
    OP("dve", lambda e: e.tensor_copy(out=A1[:, 1, :], in_=PWR[:, T8, :]), reads=[T("PWR")], partial=[CONST])
    OP("dve", lambda e: e.tensor_scalar(out=A2[:, 0, :], in0=PWI[:, T8, :], scalar1=SGN[:, 0:1], scalar2=None, op0=ALU.mult),
       reads=[T("PWI"), T("SGN")], partial=[CONST])
    OP("dve", lambda e: e.tensor_scalar(out=A2[:, 1, :], in0=PWI[:, T8, :], scalar1=SGN[:, 0:1], scalar2=-1.0, op0=ALU.mult, op1=ALU.mult),
       reads=[T("PWI"), T("SGN")], partial=[CONST])
    S5W = Buf()
    for i, (src, sname) in enumerate(((Mm, "Mm"), (P1, "P1"), (P2, "P2"), (QAo, "QAo"))):
        DMA("sp", lambda e, i=i, src=src: e.dma_start(out=s5scr[i].ap(), in_=src[:].rearrange("p g n -> p (g n)")), "s5st",
            reads=[T(sname)], partial=[S5W])
    setup_bufs = list(sb.values())

    def ring_load(srcs, extra=()):
        slot = ringi[0]
        ringi[0] = (ringi[0] + 1) % 4
        rb = ringB[slot]
        rb.start_gen()
        for (doff, ddims, src) in srcs:
            DMA("pool", lambda e, doff=doff, ddims=ddims, src=src, slot=slot: e.dma_start(out=sap(ring, slot * 4096 + doff, ddims), in_=src),
                f"ring{slot}", partial=[rb], extra=extra)
        return slot, rb

    def wsrc(dram, row0, col0, ncols, nrowchunks=8, rowlen=None):
        rl = rowlen if rowlen is not None else dram.shape[-1]
        return bass.AP(dram, row0 * rl + col0, [[rl, 128], [128 * rl, nrowchunks], [1, ncols]])

    presq = [False]
    HOOKS = [upto == "all"]

    def square_row(KT, j):
        if not HOOKS[0]:
            return
        if not presq[0]:
            presq[0] = True
            B["ssq"].start_gen()
            B["hs"].start_gen()
        OP("act", lambda e, j=j: e.activation(out=hs[0:KT, j * D:(j + 1) * D], in_=h[0:KT, j, :], func=AF.Square, accum_out=ssq[0:KT, j:j + 1]),
           reads=[B["h"]], partial=[B["ssq"], B["hs"]])

    def rmsnorm_to_hs(KT, perm):
        if presq[0]:
            presq[0] = False
        else:
            B["ssq"].start_gen()
            B["hs"].start_gen()
            for j in range(8):
                OP("act", lambda e, j=j: e.activation(out=hs[0:KT, j * D:(j + 1) * D], in_=h[0:KT, j, :], func=AF.Square, accum_out=ssq[0:KT, j:j + 1]),
                   reads=[B["h"]], partial=[B["ssq"], B["hs"]])
        OP("act", lambda e: e.activation(out=rstd[0:KT, :], in_=ssq[0:KT, :], func=AF.Ln, scale=1.0 / D, bias=EPS), reads=[B["ssq"]], writes=[B["rstd"]])
        OP("act", lambda e: e.activation(out=rstd[0:KT, :], in_=rstd[0:KT, :], func=AF.Exp, scale=-0.5), reads=[B["rstd"]], writes=[B["rstd"]])
        B["hs"].start_gen()
        for j in range(8):
            if perm:
                outap = sap(hs, j * 16, [[128, G], [1, 16]], parts=KT)
                inap = h[0:KT, j, :].rearrange("p (g c) -> p g c", c=16)
            else:
                outap = hs[0:KT, j * D:(j + 1) * D]
                inap = h[0:KT, j, :]
            if j % 2 == 0:
                OP("dve", lambda e, outap=outap, inap=inap, j=j: e.tensor_scalar(out=outap, in0=inap, scalar1=rstd[0:KT, j:j + 1], scalar2=None, op0=ALU.mult),
                   reads=[B["h"], B["rstd"]], partial=[B["hs"]])
            else:
                OP("act", lambda e, outap=outap, inap=inap, j=j: e.activation(out=outap, in_=inap, func=AF.Copy, scale=rstd[0:KT, j:j + 1]),
                   reads=[B["h"], B["rstd"]], partial=[B["hs"]])

    def transposes_to_fm(KT, gi, dst, dstB, gi2=None, dst2=None, dstB2=None):
        NT = 8 * KT
        dstB.start_gen()
        if dst2 is not None:
            dstB2.start_gen()
        cnt = 0
        for fc in range(FC):
            for jh in range(2):
                b = next_bank()
                pbuf[b].start_gen()
                for jj in range(4):
                    j = 4 * jh + jj
                    OP("pe", lambda e, b=b, j=j, jj=jj, fc=fc: e.matmul(PB[b][:, jj * KT:(jj + 1) * KT], lhsT=hs[0:KT, j * D + fc * 128:j * D + fc * 128 + 128],
                                                                      rhs=ident[0:KT, 0:KT], start=True, stop=True),
                       reads=[B["hs"], CONST], partial=[pbuf[b]])
                outap = sap(dst, fc * 1024 + 4 * jh, [[1, 4], [8, KT]])
                inap = PB[b].ap(0, [[KT, 4], [1, KT]], parts=128)
                if cnt % 2 == 0:
                    if gi is None:
                        OP("dve", lambda e, outap=outap, inap=inap: e.tensor_copy(out=outap, in_=inap), reads=[pbuf[b]], partial=[dstB])
                    else:
                        OP("dve", lambda e, outap=outap, inap=inap, fc=fc: e.tensor_scalar(out=outap, in0=inap, scalar1=gcol[:, gi, fc:fc + 1], scalar2=None,
                                                                                        op0=ALU.mult), reads=[pbuf[b], CONST], partial=[dstB])
                else:
                    if gi is None:
                        OP("act", lambda e, outap=outap, inap=inap: e.activation(out=outap, in_=inap, func=AF.Copy), reads=[pbuf[b]], partial=[dstB])
                    else:
                        OP("act", lambda e, outap=outap, inap=inap, fc=fc: e.activation(out=outap, in_=inap, func=AF.Copy, scale=gcol[:, gi, fc:fc + 1]),
                           reads=[pbuf[b], CONST], partial=[dstB])
                if dst2 is not None:
                    outap2 = sap(dst2, fc * 1024 + 4 * jh, [[1, 4], [8, KT]])
                    if cnt % 2 == 0:
                        OP("dve", lambda e, outap2=outap2, inap=inap, fc=fc: e.tensor_scalar(out=outap2, in0=inap, scalar1=gcol[:, gi2, fc:fc + 1], scalar2=None,
                                                                                          op0=ALU.mult), reads=[pbuf[b], CONST], partial=[dstB2])
                    else:
                        OP("act", lambda e, outap2=outap2, inap=inap, fc=fc: e.activation(out=outap2, in_=inap, func=AF.Copy, scale=gcol[:, gi2, fc:fc + 1]),
                           reads=[pbuf[b], CONST], partial=[dstB2])
                cnt += 1

    def alias_fence(new_bufs, old_bufs):
        deps = []
        for ob in old_bufs:
            deps += ob.wr()
        for nb in new_bufs:
            for d in deps:
                nb.r[("al", id(d))] = d

    def colgroups(NT):
        return [(c0, min(512, NT - c0)) for c0 in range(0, NT, 512)]

    def dbg_dump_h(KT, tok0):
        if dbg_d is None:
            return
        DMA("sp", lambda e: e.dma_start(out=bass.AP(dbg_d, tok0 * D, [[8 * D, KT], [1, 8 * D]]), in_=h[0:KT].rearrange("p j f -> p (j f)")),
            "dbg", reads=[B["h"]])

    def s5_layer(KT):
        NT = 8 * KT
        rmsnorm_to_hs(KT, perm=True)
        B["U"].start_gen()
        for g0 in range(0, G, 4):
            b = next_bank()
            pbuf[b].start_gen()
            for gg in range(4):
                g = g0 + gg
                OP("pe", lambda e, b=b, g=g, gg=gg: e.matmul(PB[b][:, gg * KT:(gg + 1) * KT], lhsT=hs[0:KT, g * 128:(g + 1) * 128], rhs=ident[0:KT, 0:KT],
                                                           start=True, stop=True), reads=[B["hs"], CONST], partial=[pbuf[b]])
            OP("dve", lambda e, b=b, g0=g0: e.tensor_tensor(out=U[:, g0:g0 + 4, 0:KT], in0=PB[b].ap(0, [[KT, 4], [1, KT]], parts=128),
                                                           in1=sap(gcolS5, g0, [[1, 4], [0, KT]]), op=ALU.mult),
               reads=[pbuf[b], CONST], partial=[B["U"]])
        segs = [(k0, min(64, KT - k0)) for k0 in range(0, KT, 64)]
        B["Xb"].start_gen()
        for (k0, kn) in segs:
            B["Wbuf"].start_gen()
            for lay in range(2):
                for gh in range(2):
                    slot, rb = ring_load([(0, [[1, 4096]], bass.AP(s5scr[1 + lay], gh * 4096, [[G * 128, 128], [1, 4096]]))], extra=S5W.rd())
                    for g8 in range(0, 32, 8):
                        b = next_bank()
                        pbuf[b].start_gen()
                        for gg in range(8):
                            gl = g8 + gg
                            g = gh * 32 + gl
                            OP("pe", lambda e, b=b, g=g, gl=gl, gg=gg, slot=slot, kn=kn, k0=k0: e.matmul(
                                PB[b][:, gg * kn:(gg + 1) * kn], lhsT=ring[:, slot, gl * 128:(gl + 1) * 128], rhs=U[:, g, k0:k0 + kn], start=True, stop=True),
                               reads=[rb, B["U"]], partial=[pbuf[b]])
                        gbase = gh * 32 + g8
                        eng = "act" if (g8 // 8) % 2 == 0 else "dve"
                        outap = sap(Wbuf, lay * G + gbase, [[1, 8], [2 * G, kn]])
                        inap = PB[b].ap(0, [[kn, 8], [1, kn]], parts=128)
                        if eng == "act":
                            OP("act", lambda e, outap=outap, inap=inap: e.activation(out=outap, in_=inap, func=AF.Copy), reads=[pbuf[b]], partial=[B["Wbuf"]])
                        else:
                            OP("dve", lambda e, outap=outap, inap=inap: e.tensor_copy(out=outap, in_=inap), reads=[pbuf[b]], partial=[B["Wbuf"]])
            if dbg_d is not None and KT == 128 and dbgcnt[0] == 0:
                DMA("sp", lambda e: e.dma_start(out=dbgW0.ap(), in_=Wbuf[:].rearrange("p a g k -> p (a g k)")), "dbgw", reads=[B["Wbuf"]])
                DMA("sp", lambda e: e.dma_start(out=dbgU.ap(), in_=U[:].rearrange("p g k -> p (g k)")), "dbgw", reads=[B["U"]])
            OP("act", lambda e, k0=k0: e.activation(out=sap(Xb, k0, [[128, G]]), in_=Zc[:, 0, :], func=AF.Copy), reads=[B["Zc"]], partial=[B["Xb"]])
            SCB = 4
            pA1 = PB[SCB].ap(0, [[G, 2], [1, G]])
            pA2 = PB[SCB].ap(128, [[G, 2], [1, G]])
            pT2 = PB[SCB].ap(256, [[G, 2], [1, G]])
            pS = PB[SCB].ap(384, [[G, 2], [1, G]])
            if k0 == 0:
                pbuf[SCB].start_gen()
                OP("dve", lambda e: e.tensor_copy(out=pA1, in_=A1[:]), reads=[CONST], partial=[pbuf[SCB]])
                OP("dve", lambda e: e.tensor_copy(out=pA2, in_=A2[:]), reads=[CONST], partial=[pbuf[SCB]])
            pcon = pbuf[SCB]
            for k in range(kn):
                if k == 0:
                    zfull = Zc[:]
                    zswap = sap(Zc, G, [[-G, 2], [1, G]])
                    zb = B["Zc"]
                else:
                    zfull = sap(Wbuf, (k - 1) * 2 * G, [[G, 2], [1, G]])
                    zswap = sap(Wbuf, (k - 1) * 2 * G + G, [[-G, 2], [1, G]])
                    zb = B["Wbuf"]
                wk = sap(Wbuf, k * 2 * G, [[G, 2], [1, G]])
                OP("dve", lambda e, zfull=zfull: e.tensor_tensor(out=sT1[:], in0=pA1, in1=zfull, op=ALU.mult), reads=[zb, pcon], writes=[B["sT1"]])
                OP("dve", lambda e, zswap=zswap: e.tensor_tensor(out=pT2, in0=pA2, in1=zswap, op=ALU.mult), reads=[zb, pcon], writes=[B["sT2"]])
                OP("dve", lambda e: e.tensor_tensor(out=pS, in0=pT2, in1=sT1[:], op=ALU.add), reads=[B["sT1"], B["sT2"]], writes=[B["sT2"]])
                o = OP("dve", lambda e, wk=wk: e.tensor_tensor(out=wk, in0=pS, in1=wk, op=ALU.add), reads=[B["sT2"], B["Wbuf"]], writes=[],
                       extra=list(B["Wbuf"].r.values()))
                B["Wbuf"].also_wrote(o)
            if kn > 1:
                OP("act", lambda e, k0=k0, kn=kn: e.activation(out=Xb[:, :, k0 + 1:k0 + kn], in_=sap(Wbuf, 0, [[1, G], [2 * G, kn - 1]]), func=AF.Copy),
                   reads=[B["Wbuf"]], partial=[B["Xb"]])
            OP("dve", lambda e, kn=kn: e.tensor_copy(out=Zc[:], in_=sap(Wbuf, (kn - 1) * 2 * G, [[G, 2], [1, G]])), reads=[B["Wbuf"]], writes=[B["Zc"]])
        rotmod[0] = 8
        B["hs"].start_gen()
        for gh in range(2):
            slotM, rbM = ring_load([(0, [[1, 4096]], bass.AP(s5scr[0], gh * 4096, [[G * 128, 128], [1, 4096]]))], extra=S5W.rd())
            slotQ, rbQ = ring_load([(0, [[1, 4096]], bass.AP(s5scr[3], gh * 4096, [[G * 128, 128], [1, 4096]]))], extra=S5W.rd())
            for g4 in range(0, 32, 4):
                b = next_bank()
                pbuf[b].start_gen()
                for gg in range(4):
                    gl = g4 + gg
                    g = gh * 32 + gl
                    OP("pe", lambda e, b=b, g=g, gl=gl, gg=gg, slotM=slotM: e.matmul(PB[b][0:KT, gg * 128:(gg + 1) * 128], lhsT=U[:, g, 0:KT],
                                                                                   rhs=ring[:, slotM, gl * 128:(gl + 1) * 128], start=True, stop=False),
                       reads=[rbM, B["U"]], partial=[pbuf[b]])
                    OP("pe", lambda e, b=b, g=g, gl=gl, gg=gg, slotQ=slotQ: e.matmul(PB[b][0:KT, gg * 128:(gg + 1) * 128], lhsT=Xb[:, g, 0:KT],
                                                                                   rhs=ring[:, slotQ, gl * 128:(gl + 1) * 128], start=False, stop=True),
                       reads=[rbQ, B["Xb"]], partial=[pbuf[b]])
                gbase = gh * 32 + g4
                outap = sap(hs, 16 * gbase, [[16, 4], [D, 8], [1, 16]], parts=KT)
                inap = PB[b].ap(0, [[128, 4], [16, 8], [1, 16]], parts=KT)
                OP("act", lambda e, outap=outap, inap=inap: e.activation(out=outap, in_=inap, func=AF.Gelu_apprx_tanh), reads=[pbuf[b]], partial=[B["hs"]])
        transposes_to_fm(KT, None, xnT, B["xnT"])
        for q in range(2):
            slot, rb = ring_load([(0, [[512, 8], [1, 512]], wsrc(wglu_d, 0, D + 512 * q, 512))])
            B["sgb"].start_gen()
            for j in range(8):
                b = next_bank()
                pbuf[b].start_gen()
                for fc in range(FC):
                    OP("pe", lambda e, b=b, j=j, fc=fc, slot=slot: e.matmul(PB[b][0:KT, :], lhsT=sap(xnT, fc * 1024 + j, [[8, KT]]),
                                                                          rhs=ring[:, slot, fc * 512:(fc + 1) * 512], start=(fc == 0), stop=(fc == FC - 1)),
                       reads=[rb, B["xnT"]], partial=[pbuf[b]])
                OP("act", lambda e, b=b, j=j: e.activation(out=sgb[0:KT, j, :], in_=PB[b][0:KT, :], func=AF.Sigmoid), reads=[pbuf[b]], partial=[B["sgb"]])
            slot, rb = ring_load([(0, [[512, 8], [1, 512]], wsrc(wglu_d, 0, 512 * q, 512))])
            for j in range(8):
                b = next_bank()
                pbuf[b].start_gen()
                for fc in range(FC):
                    OP("pe", lambda e, b=b, j=j, fc=fc, slot=slot: e.matmul(PB[b][0:KT, :], lhsT=sap(xnT, fc * 1024 + j, [[8, KT]]),
                                                                          rhs=ring[:, slot, fc * 512:(fc + 1) * 512], start=(fc == 0), stop=(fc == FC - 1)),
                       reads=[rb, B["xnT"]], partial=[pbuf[b]])
                o = OP("dve", lambda e, b=b, j=j: e.tensor_tensor(out=sgb[0:KT, j, :], in0=PB[b][0:KT, :], in1=sgb[0:KT, j, :], op=ALU.mult),
                       reads=[pbuf[b], B["sgb"]])
                B["sgb"].also_wrote(o)
                o = OP("dve", lambda e, j=j, q=q: e.tensor_tensor(out=h[0:KT, j, 512 * q:512 * q + 512], in0=h[0:KT, j, 512 * q:512 * q + 512],
                                                                  in1=sgb[0:KT, j, :], op=ALU.add), reads=[B["sgb"], B["h"]], extra=B["h"].wr())
                B["h"].also_wrote(o)
                if q == 1:
                    square_row(KT, j)
        rotmod[0] = 4
        rot[0] = 0

    def ffn_layer(KT, li, gi):
        NT = 8 * KT
        rmsnorm_to_hs(KT, perm=False)
        transposes_to_fm(KT, gi, xnT, B["xnT"])
        cgs = colgroups(NT)
        rotmod[0] = 8
        B["act"].start_gen()
        B["woutb"].start_gen()
        for m in range(MF):
            slot, rb = ring_load([
                (0, [[256, 8], [1, 128]], bass.AP(win_d, li * D * 2 * DFF + m * 128, [[2 * DFF, 128], [128 * 2 * DFF, 8], [1, 128]])),
                (128, [[256, 8], [1, 128]], bass.AP(win_d, li * D * 2 * DFF + DFF + m * 128, [[2 * DFF, 128], [128 * 2 * DFF, 8], [1, 128]]))])
            DMA("pool", lambda e, m=m: e.dma_start(out=woutb[:, m, :], in_=bass.AP(wout_d, li * DFF * D + m * 128 * D, [[D, 128], [1, D]])),
                "wout", partial=[B["woutb"]])
            for ci, (c0, cn) in enumerate(cgs):
                bg = next_bank()
                pbuf[bg].start_gen()
                for fc in range(FC):
                    OP("pe", lambda e, bg=bg, fc=fc, slot=slot, c0=c0, cn=cn: e.matmul(PB[bg][:, 0:cn], lhsT=ring[:, slot, fc * 256:fc * 256 + 128],
                                                                                     rhs=xnT[:, fc, c0:c0 + cn], start=(fc == 0), stop=(fc == FC - 1)),
                       reads=[rb, B["xnT"]], partial=[pbuf[bg]])
                bu = next_bank()
                pbuf[bu].start_gen()
                for fc in range(FC):
                    OP("pe", lambda e, bu=bu, fc=fc, slot=slot, c0=c0, cn=cn: e.matmul(PB[bu][:, 0:cn], lhsT=ring[:, slot, fc * 256 + 128:fc * 256 + 256],
                                                                                     rhs=xnT[:, fc, c0:c0 + cn], start=(fc == 0), stop=(fc == FC - 1)),
                       reads=[rb, B["xnT"]], partial=[pbuf[bu]])
                sl = (m * 2 + ci) % 2
                sB = B[f"silu{sl}"]
                OP("act", lambda e, bg=bg, sl=sl, cn=cn: e.activation(out=silu_t[:, sl, 0:cn], in_=PB[bg][:, 0:cn], func=AF.Silu), reads=[pbuf[bg]], writes=[sB])
                OP("dve", lambda e, bu=bu, sl=sl, m=m, c0=c0, cn=cn: e.tensor_tensor(out=act[:, m, c0:c0 + cn], in0=PB[bu][:, 0:cn], in1=silu_t[:, sl, 0:cn],
                                                                                   op=ALU.mult), reads=[pbuf[bu], sB], partial=[B["act"]])
        for j in range(8):
            for half in range(2):
                b = next_bank()
                pbuf[b].start_gen()
                for m in range(MF):
                    OP("pe", lambda e, b=b, j=j, m=m, half=half: e.matmul(PB[b][0:KT, :], lhsT=sap(act, m * 1024 + j, [[8, KT]]),
                                                                        rhs=woutb[:, m, 512 * half:512 * half + 512], start=(m == 0), stop=(m == MF - 1)),
                       reads=[B["act"], B["woutb"]], partial=[pbuf[b]])
                o = OP("dve", lambda e, b=b, j=j, half=half: e.tensor_tensor(out=h[0:KT, j, 512 * half:512 * half + 512], in0=PB[b][0:KT, :],
                                                                            in1=h[0:KT, j, 512 * half:512 * half + 512], op=ALU.add),
                       reads=[pbuf[b], B["h"]], extra=B["h"].wr())
                B["h"].also_wrote(o)
                if half == 1:
                    square_row(KT, j)

    _ffn_inner = ffn_layer

    def ffn_layer(KT, li, gi):
        try:
            _ffn_inner(KT, li, gi)
        finally:
            rotmod[0] = 4
            rot[0] = 0

    KVB = Buf()

    def kv_proj(KT, tok0):
        NT = 8 * KT
        rmsnorm_to_hs(KT, perm=False)
        if KT == 128:
            transposes_to_fm(KT, 2, xnT, B["xnT"], 3, xnT2, B["xnT2"])
        else:
            transposes_to_fm(KT, 2, xnT, B["xnT"])
        cgs = colgroups(NT)
        rotmod[0] = 8
        B["Kst"].start_gen()
        cnt = 0
        for eh in range(2):
            slot, rb = ring_load([(0, [[512, 8], [1, 512]], wsrc(wkv_d, 0, 512 * eh, 512))])
            for e4 in range(4):
                ech = eh * 4 + e4
                for (c0, cn) in cgs:
                    b = next_bank()
                    pbuf[b].start_gen()
                    for fc in range(FC):
                        OP("pe", lambda e, b=b, fc=fc, slot=slot, e4=e4, c0=c0, cn=cn: e.matmul(PB[b][:, 0:cn], lhsT=ring[:, slot, fc * 512 + e4 * 128:fc * 512 + e4 * 128 + 128],
                                                                                              rhs=xnT[:, fc, c0:c0 + cn], start=(fc == 0), stop=(fc == FC - 1)),
                           reads=[rb, B["xnT"]], partial=[pbuf[b]])
                    if cnt % 2 == 0:
                        OP("act", lambda e, b=b, ech=ech, c0=c0, cn=cn: e.activation(out=Kst[:, ech, c0:c0 + cn], in_=PB[b][:, 0:cn], func=AF.Copy),
                           reads=[pbuf[b]], partial=[B["Kst"]])
                    else:
                        OP("dve", lambda e, b=b, ech=ech, c0=c0, cn=cn: e.tensor_copy(out=Kst[:, ech, c0:c0 + cn], in_=PB[b][:, 0:cn]),
                           reads=[pbuf[b]], partial=[B["Kst"]])
                    cnt += 1
        DMA("sp", lambda e: e.dma_start(out=bass.AP(KT_all, tok0, [[L, 128], [128 * L, 8], [1, NT]]), in_=Kst[:, :, 0:NT]), "kvst",
            reads=[B["Kst"]], partial=[KVB])
        B["Vst"].start_gen()
        tbs = [(t0, min(128, NT - t0)) for t0 in range(0, NT, 128)]
        for vh in range(2):
            slot, rb = ring_load([(0, [[512, 8], [1, 512]], wsrc(wkv_d, 0, D + 512 * vh, 512))])
            for ti, (t0, tn) in enumerate(tbs):
                b = next_bank()
                pbuf[b].start_gen()
                for fc in range(FC):
                    OP("pe", lambda e, b=b, fc=fc, slot=slot, t0=t0, tn=tn: e.matmul(PB[b][0:tn, :], lhsT=xnT[:, fc, t0:t0 + tn],
                                                                                   rhs=ring[:, slot, fc * 512:(fc + 1) * 512], start=(fc == 0), stop=(fc == FC - 1)),
                       reads=[rb, B["xnT"]], partial=[pbuf[b]])
                if cnt % 2 == 0:
                    OP("act", lambda e, b=b, ti=ti, tn=tn, vh=vh: e.activation(out=Vst[0:tn, ti, 512 * vh:512 * vh + 512], in_=PB[b][0:tn, :], func=AF.Copy),
                       reads=[pbuf[b]], partial=[B["Vst"]])
                else:
                    OP("dve", lambda e, b=b, ti=ti, tn=tn, vh=vh: e.tensor_copy(out=Vst[0:tn, ti, 512 * vh:512 * vh + 512], in_=PB[b][0:tn, :]),
                       reads=[pbuf[b]], partial=[B["Vst"]])
                cnt += 1
        if NT >= 128:
            DMA("sp", lambda e: e.dma_start(out=bass.AP(V_all, tok0 * D, [[D, 128], [128 * D, NT // 128], [1, D]]), in_=Vst[:, 0:NT // 128, :]), "kvst",
                reads=[B["Vst"]], partial=[KVB])
        else:
            DMA("sp", lambda e: e.dma_start(out=bass.AP(V_all, tok0 * D, [[D, NT], [1, D]]), in_=Vst[0:NT, 0, :]), "kvst",
                reads=[B["Vst"]], partial=[KVB])
        rotmod[0] = 4
        rot[0] = 0

    kvslotB = [Buf(), Buf()]
    tmpB = {n: [Buf(), Buf()] for n in ("E", "SP", "ARG", "LM", "W")}
    ABANK = [4, 5]
    OBANK = [6, 7]
    chain_ctr = [0]
    kvl_ctr = [0]

    def attention(a, tok0):
        KT = 128
        NT = 1024
        B["QT"].start_gen()
        cnt = 0
        for eh in range(2):
            slot, rb = ring_load([(0, [[512, 8], [1, 512]], wsrc(wq_d, 0, 512 * eh, 512))])
            for e4 in range(4):
                ech = eh * 4 + e4
                for (c0, cn) in colgroups(NT):
                    b = next_bank()
                    pbuf[b].start_gen()
                    for fc in range(FC):
                        OP("pe", lambda e, b=b, fc=fc, slot=slot, e4=e4, c0=c0, cn=cn: e.matmul(PB[b][:, 0:cn], lhsT=ring[:, slot, fc * 512 + e4 * 128:fc * 512 + e4 * 128 + 128],
                                                                                              rhs=xnT2[:, fc, c0:c0 + cn], start=(fc == 0), stop=(fc == FC - 1)),
                           reads=[rb, B["xnT2"]], partial=[pbuf[b]])
                    if cnt % 2 == 0:
                        OP("act", lambda e, b=b, ech=ech, c0=c0, cn=cn: e.activation(out=QT[:, ech, c0:c0 + cn], in_=PB[b][:, 0:cn], func=AF.Copy),
                           reads=[pbuf[b]], partial=[B["QT"]])
                    else:
                        OP("dve", lambda e, b=b, ech=ech, c0=c0, cn=cn: e.tensor_copy(out=QT[:, ech, c0:c0 + cn], in_=PB[b][:, 0:cn]),
                           reads=[pbuf[b]], partial=[B["QT"]])
                    cnt += 1
        alias_fence([t_ for n_ in tmpB for t_ in tmpB[n_]], [B["xnT2"]])
        nk = NMETA + 1024 * (a + 1)
        nrb = 8 * (a + 1)
        B["oT"].start_gen()

        nfull = 8 * (a + 1)

        def load_kv(hp):
            ks = hp % 2
            kb_ = kvslotB[ks]
            kb_.start_gen()
            DMA("sp", lambda e, hp=hp, ks=ks: e.dma_start(out=KTp[:, ks, 0:nk], in_=bass.AP(KT_all, hp * 128 * L, [[L, 128], [1, nk]])), f"kvl{ks}",
                partial=[kb_], extra=KVB.rd())
            DMA("sp", lambda e, hp=hp, ks=ks: e.dma_start(out=sap(Vp, ks * 33 * 128, [[128, nfull], [1, 128]]),
                                                         in_=bass.AP(V_all, hp * 128, [[D, 128], [128 * D, nfull], [1, 128]])), f"kvl{ks}",
                partial=[kb_], extra=KVB.rd())
            DMA("sp", lambda e, hp=hp, ks=ks: e.dma_start(out=Vp[0:NMETA, ks, nfull * 128:nfull * 128 + 128],
                                                         in_=bass.AP(V_all, 128 * nfull * D + hp * 128, [[D, NMETA], [1, 128]])), f"kvl{ks}",
                partial=[kb_], extra=KVB.rd())

        steps = []
        for hp in range(8):
            for qs in range(2):
                nbq = 4 * (2 * a + qs + 1)
                blocks = [("tail", nbq, 4)] + [("full", b_, b_ - (nbq - 4)) for b_ in range(nbq - 1, -1, -1)]
                for bi, (kind, rbi, dd) in enumerate(blocks):
                    steps.append(dict(hp=hp, qs=qs, bi=bi, nblk=len(blocks), kind=kind, rbi=rbi, dd=dd, ci=hp * 2 + qs,
                                      first_of_hp=(qs == 0 and bi == 0)))
        NS = len(steps)
        APAIR = 4
        a_read = [None]

        def geom(st):
            kp = NMETA if st["kind"] == "tail" else 128
            kc0 = 128 * st["rbi"]
            vcol = 128 * st["rbi"]
            diag = st["dd"] >= 0
            c0 = max(0, 128 * st["dd"] - 16) if diag else 0
            return kp, kc0, vcol, diag, c0

        def maskap(st, kp):
            if st["dd"] == 0:
                return 128, sap(dmask, 512, [[0, 2], [1, 128]], parts=kp)
            wid = 16 if st["kind"] == "tail" else 128
            return wid, sap(dmask, 0, [[0, 2], [1, wid]], parts=kp)

        def wv(t, sl, kp, c0, wid=None):
            wid = 512 - c0 if wid is None else wid
            return sap(t, sl * 1024 + c0, [[512, 2], [1, wid]], parts=kp)

        def zpair(sl, kp, c0):
            return bass.AP(PBt, 1024 * sl + c0, [[4096, kp], [512, 2], [1, 512 - c0]])

        def apair(kp, c0):
            return bass.AP(PBt, 512 * APAIR + c0, [[4096, kp], [512, 2], [1, 512 - c0]])

        def st1(T):
            st = steps[T]
            sl = T % 2
            kp, kc0, vcol, diag, c0 = geom(st)
            ks, hp, q0 = st["hp"] % 2, st["hp"], 512 * st["qs"]
            for hh in range(2):
                zb = 2 * sl + hh
                pbuf[zb].start_gen()
                prs = slice(64 * hh, 64 * hh + 64)
                OP("pe", lambda e, zb=zb, kp=kp, kc0=kc0, ks=ks, prs=prs, hp=hp, q0=q0, c0=c0: e.matmul(
                    PB[zb][0:kp, c0:512], lhsT=KTp[prs, ks, kc0:kc0 + kp], rhs=QT[prs, hp, q0 + c0:q0 + 512], start=True, stop=True),
                   reads=[kvslotB[ks], B["QT"]], partial=[pbuf[zb]])

        def st2(T):
            st = steps[T]
            sl = T % 2
            kp, kc0, vcol, diag, c0 = geom(st)
            OP("act", lambda e, sl=sl, kp=kp, c0=c0: e.activation(out=wv(Et, sl, kp, c0), in_=zpair(sl, kp, c0), func=AF.Exp, scale=-0.125),
               reads=[pbuf[2 * sl], pbuf[2 * sl + 1]], writes=[tmpB["E"][sl]])
            OP("act", lambda e, sl=sl, kp=kp, c0=c0: e.activation(out=wv(SPt, sl, kp, c0), in_=wv(Et, sl, kp, c0), func=AF.Ln, bias=1.0),
               reads=[tmpB["E"][sl]], writes=[tmpB["SP"][sl]])

        def st3(T):
            st = steps[T]
            sl = T % 2
            kp, kc0, vcol, diag, c0 = geom(st)
            OP("dve", lambda e, sl=sl, kp=kp, c0=c0: e.scalar_tensor_tensor(out=wv(LMt, sl, kp, c0), in0=zpair(sl, kp, c0), scalar=-0.125,
                                                                          in1=wv(SPt, sl, kp, c0), op0=ALU.mult, op1=ALU.subtract),
               reads=[pbuf[2 * sl], pbuf[2 * sl + 1], tmpB["SP"][sl]], writes=[tmpB["LM"][sl]])
            if diag:
                mw, mk = maskap(st, kp)
                OP("dve", lambda e, sl=sl, kp=kp, c0=c0, mw=mw, mk=mk: e.tensor_tensor(out=wv(LMt, sl, kp, c0, mw), in0=wv(LMt, sl, kp, c0, mw),
                                                                                     in1=mk, op=ALU.mult),
                   reads=[tmpB["LM"][sl], CONST], writes=[tmpB["LM"][sl]])

        def st4(T):
            st = steps[T]
            sl = T % 2
            kp, kc0, vcol, diag, c0 = geom(st)
            first = st["bi"] == 0
            for hh in range(2):
                ab = APAIR + hh
                if first:
                    pbuf[ab].start_gen()
                extra = [a_read[0]] if a_read[0] is not None else []
                OP("pe", lambda e, ab=ab, kp=kp, sl=sl, hh=hh, first=first, c0=c0: e.matmul(
                    PB[ab][:, c0:512], lhsT=TU[0:kp, :], rhs=sap(LMt, sl * 1024 + 512 * hh + c0, [[1, 512 - c0]], parts=kp),
                    start=first, stop=False, skip_group_check=True),
                   reads=[tmpB["LM"][sl], CONST], partial=[pbuf[ab]], extra=extra)

        def st5(T):
            st = steps[T]
            sl = T % 2
            kp, kc0, vcol, diag, c0 = geom(st)
            a_read[0] = OP("dve", lambda e, sl=sl, kp=kp, c0=c0: e.tensor_tensor(out=wv(ARGt, sl, kp, c0), in0=apair(kp, c0),
                                                                               in1=wv(SPt, sl, kp, c0), op=ALU.subtract),
                           reads=[pbuf[APAIR], pbuf[APAIR + 1], tmpB["SP"][sl]], writes=[tmpB["ARG"][sl]])
            pbuf[APAIR].did_read(a_read[0])
            pbuf[APAIR + 1].did_read(a_read[0])

        def st6(T):
            st = steps[T]
            sl = T % 2
            kp, kc0, vcol, diag, c0 = geom(st)
            if st["bi"] == st["nblk"] - 1:
                return
            for hh in range(2):
                ab = APAIR + hh
                OP("pe", lambda e, ab=ab, kp=kp, sl=sl, hh=hh, c0=c0: e.matmul(
                    PB[ab][:, c0:512], lhsT=TL[0:kp, :], rhs=sap(LMt, sl * 1024 + 512 * hh + c0, [[1, 512 - c0]], parts=kp),
                    start=False, stop=False, skip_group_check=True),
                   reads=[tmpB["LM"][sl], CONST], partial=[pbuf[ab]], extra=[a_read[0]])

        def st7(T):
            st = steps[T]
            sl = T % 2
            kp, kc0, vcol, diag, c0 = geom(st)
            OP("act", lambda e, sl=sl, kp=kp, c0=c0: e.activation(out=wv(Wtt, sl, kp, c0), in_=wv(ARGt, sl, kp, c0), func=AF.Exp),
               reads=[tmpB["ARG"][sl]], writes=[tmpB["W"][sl]])
            if diag:
                mw, mk = maskap(st, kp)
                OP("dve", lambda e, sl=sl, kp=kp, c0=c0, mw=mw, mk=mk: e.tensor_tensor(out=wv(Wtt, sl, kp, c0, mw), in0=wv(Wtt, sl, kp, c0, mw),
                                                                                     in1=mk, op=ALU.mult),
                   reads=[tmpB["W"][sl], CONST], writes=[tmpB["W"][sl]])

        def st8(T):
            st = steps[T]
            sl = T % 2
            kp, kc0, vcol, diag, c0 = geom(st)
            ks, hp, q0 = st["hp"] % 2, st["hp"], 512 * st["qs"]
            ob = 6 + st["ci"] % 2
            first = st["bi"] == 0
            last = st["bi"] == st["nblk"] - 1
            if first:
                pbuf[ob].start_gen()
            for hh in range(2):
                OP("pe", lambda e, ob=ob, kp=kp, sl=sl, ks=ks, vcol=vcol, hh=hh, first=first, last=last, c0=c0: e.matmul(
                    PB[ob][64 * hh:64 * hh + 64, c0:512], lhsT=Vp[0:kp, ks, vcol + 64 * hh:vcol + 64 * hh + 64],
                    rhs=sap(Wtt, sl * 1024 + 512 * hh + c0, [[1, 512 - c0]], parts=kp), start=first, stop=last, skip_group_check=True),
                   reads=[tmpB["W"][sl], kvslotB[ks]], partial=[pbuf[ob]])
            if last:
                OP("act", lambda e, ob=ob, hp=hp, q0=q0: e.activation(out=oT[:, hp, q0:q0 + 512], in_=PB[ob][:, :], func=AF.Copy),
                   reads=[pbuf[ob]], partial=[B["oT"]])

        load_kv(0)
        load_kv(1)
        hp_first_T = {}
        for T, st in enumerate(steps):
            if st["first_of_hp"]:
                hp_first_T[st["hp"]] = T
        st1(0)
        for T in range(NS + 3):
            if 0 <= T - 1 < NS:
                st4(T - 1)
                st5(T - 1)
            if T + 1 < NS:
                st1(T + 1)
            if 0 <= T - 1 < NS:
                st6(T - 1)
            if T < NS:
                st2(T)
            if 0 <= T - 2 < NS:
                st7(T - 2)
            if T < NS:
                st3(T)
            if 0 <= T - 3 < NS:
                st8(T - 3)
            for hp in range(1, 7):
                if hp_first_T[hp] + 4 == T:
                    load_kv(hp + 1)
        for half in range(2):
            slot, rb = ring_load([(0, [[512, 8], [1, 512]], wsrc(wo_d, 0, 512 * half, 512))])
            for j in range(8):
                b = next_bank()
                pbuf[b].start_gen()
                for hp in range(8):
                    OP("pe", lambda e, b=b, j=j, hp=hp, slot=slot: e.matmul(PB[b][:, :], lhsT=sap(oT, hp * 1024 + j, [[8, 128]]),
                                                                          rhs=ring[:, slot, hp * 512:(hp + 1) * 512], start=(hp == 0), stop=(hp == 7)),
                       reads=[rb, B["oT"]], partial=[pbuf[b]])
                o = OP("dve", lambda e, b=b, j=j, half=half: e.tensor_tensor(out=h[:, j, 512 * half:512 * half + 512], in0=PB[b][:, :],
                                                                            in1=h[:, j, 512 * half:512 * half + 512], op=ALU.add),
                       reads=[pbuf[b], B["h"]], extra=B["h"].wr())
                B["h"].also_wrote(o)
                if half == 1:
                    square_row(128, j)

    def final_norm_store(a):
        KT = 128
        if presq[0]:
            presq[0] = False
        else:
            B["ssq"].start_gen()
            B["hs"].start_gen()
            for j in range(8):
                OP("act", lambda e, j=j: e.activation(out=hs[0:KT, j * D:(j + 1) * D], in_=h[0:KT, j, :], func=AF.Square, accum_out=ssq[0:KT, j:j + 1]),
                   reads=[B["h"]], partial=[B["ssq"], B["hs"]])
        OP("act", lambda e: e.activation(out=rstd[0:KT, :], in_=ssq[0:KT, :], func=AF.Ln, scale=1.0 / D, bias=EPS), reads=[B["ssq"]], writes=[B["rstd"]])
        OP("act", lambda e: e.activation(out=rstd[0:KT, :], in_=rstd[0:KT, :], func=AF.Exp, scale=-0.5), reads=[B["rstd"]], writes=[B["rstd"]])
        B["outs"].start_gen()
        for j in range(8):
            OP("dve", lambda e, j=j: e.scalar_tensor_tensor(out=outs[:, j, :], in0=h[:, j, :], scalar=rstd[:, j:j + 1], in1=gfin[:], op0=ALU.mult, op1=ALU.mult),
               reads=[B["h"], B["rstd"], CONST], partial=[B["outs"]])
        return DMA("sp", lambda e: e.dma_start(out=bass.AP(out_d, a * 1024 * D, [[8 * D, 128], [1, 8 * D]]), in_=outs[:].rearrange("p j f -> p (j f)")),
                   "outst", reads=[B["outs"]])

    out_dmas = []
    setup_done_deps = []
    for b_ in setup_bufs:
        setup_done_deps += b_.wr()
    setup_done_deps += S5W.rd()
    for nm in ("h", "hs", "xnT", "U", "Wbuf", "Xb", "act", "woutb"):
        B[nm].w = {}
        B[nm].r = {}
    fence = S.add("sp", lambda e: e.nop(), setup_done_deps)
    for nm in ("h", "hs", "xnT", "U", "Wbuf", "Xb", "act", "woutb", "Kst", "Vst", "QT", "oT", "sgb"):
        B[nm].r["sp"] = fence
    for kb in kvslotB:
        kb.r["sp"] = fence
    for n_ in tmpB:
        for t_ in tmpB[n_]:
            t_.r["sp"] = fence

    P_S5 = [B["U"], B["Wbuf"], B["Xb"], B["sgb"]]
    P_FFN = [B["act"], B["woutb"], B["silu0"], B["silu1"]]
    P_KV = [B["Kst"], B["Vst"], B["xnT2"]]
    P_ATT = [B["QT"], B["oT"]] + kvslotB + [t_ for n_ in tmpB for t_ in tmpB[n_]]
    last_phase = [None]
    tiles = [("meta", 0, 2, 0)] + [("real", a, 128, NMETA + 1024 * a) for a in range(4)]
    stages = ["s5", "ffn0", "kv", "attn", "ffn1", "all"]
    lvl = stages.index(upto)
    for (kind, a, KT, tok0) in tiles:
        if kind == "meta":
            DMA("sp", lambda e: e.dma_start(out=h[0:2].rearrange("p j f -> p (j f)"), in_=bass.AP(meta_d, 0, [[8 * D, 2], [1, 8 * D]])), "xld",
                writes=[B["h"]])
        else:
            DMA("sp", lambda e, a=a: e.dma_start(out=h[:].rearrange("p j f -> p (j f)"), in_=bass.AP(x_d, a * 1024 * D, [[8 * D, 128], [1, 8 * D]])), "xld",
                writes=[B["h"]])
        if last_phase[0] is not None:
            alias_fence(P_S5, last_phase[0])
        s5_layer(KT)
        last_phase[0] = P_S5
        if lvl >= 1:
            alias_fence(P_FFN, last_phase[0])
            ffn_layer(KT, 0, 1)
            last_phase[0] = P_FFN
        if lvl >= 2:
            alias_fence(P_KV, last_phase[0])
            kv_proj(KT, tok0)
            last_phase[0] = P_KV
        if kind == "real":
            if lvl >= 3:
                alias_fence(P_ATT, last_phase[0])
                attention(a, tok0)
                last_phase[0] = P_ATT
            if lvl >= 4:
                alias_fence(P_FFN, last_phase[0])
                ffn_layer(KT, 1, 4)
                last_phase[0] = P_FFN
            if lvl >= 5:
                alias_fence([B["outs"]], last_phase[0])
                out_dmas.append(final_norm_store(a))
                last_phase[0] = list(last_phase[0]) + [B["outs"]]
        if lvl < 5:
            dbg_dump_h(KT, tok0)
            if dbg_d is not None:
                out_dmas.append(S.ops["sp"][-1])
    S.add("sp", lambda e: e.nop(), out_dmas)
    S.emit()
    return nc


_NC_CACHE = {}


def kernel(**inputs):
    if "nc" not in _NC_CACHE:
        _NC_CACHE["nc"] = build_nc()
    nc = _NC_CACHE["nc"]
    f = lambda a: np.ascontiguousarray(np.asarray(a, dtype=np.float32))
    shared = {
        "meta_tokens": f(inputs["meta_tokens"]),
        "norm_mix": f(inputs["norm_mix"]),
        "norm_ffn": f(inputs["norm_ffn"]),
        "s5_a_re": f(inputs["s5_a_re"]).reshape(G, 64),
        "s5_a_im": f(inputs["s5_a_im"]).reshape(G, 64),
        "s5_log_dt": f(inputs["s5_log_dt"]).reshape(1, G),
        "s5_b_re": f(inputs["s5_b_re"]).reshape(G, 64, 16),
        "s5_b_im": f(inputs["s5_b_im"]).reshape(G, 64, 16),
        "s5_c_re": f(inputs["s5_c_re"]).reshape(G * 16, 64),
        "s5_c_im": f(inputs["s5_c_im"]).reshape(G * 16, 64),
        "s5_d": f(inputs["s5_d"]).reshape(1, D),
        "s5_w_glu": f(inputs["s5_w_glu"]).reshape(D, 2 * D),
        "norm_kv": f(inputs["norm_kv"]).reshape(1, D),
        "w_kv": f(inputs["w_kv"]),
        "w_q": f(inputs["w_q"]).reshape(D, D),
        "w_o": f(inputs["w_o"]).reshape(D, D),
        "w_ffn_in": f(inputs["w_ffn_in"]),
        "w_ffn_out": f(inputs["w_ffn_out"]),
        "norm_final": f(inputs["norm_final"]).reshape(1, D),
    }
    x = f(inputs["x"])
    in_maps = [dict(shared, x=x[b]) for b in range(8)]
    res = run_bass_kernel_spmd(nc, in_maps, core_ids=list(range(8)))
    return np.stack([np.asarray(r["out"], dtype=np.float32) for r in res.results], axis=0)
```

```python
import math
import numpy as np
import concourse.bass as bass
import concourse.mybir as mybir
from concourse.bass_utils import run_bass_kernel_spmd

F32 = mybir.dt.float32
BF16 = mybir.dt.bfloat16
I32 = mybir.dt.int32
AF = mybir.ActivationFunctionType
ALU = mybir.AluOpType

D = 1024
FC = 8
SEQ = 4096
NMETA = 16
L = SEQ + NMETA
G = 64
DFF = 2816
MF = DFF // 128
NH = 16
EPS = 1e-6
ENGS = ("pe", "act", "dve", "pool", "sp")
EPOCH = 12000


class Op:
    __slots__ = ("eng", "fn", "deps", "sig", "seq", "key", "kval", "is_dma")

    def __init__(self, eng, fn, deps, is_dma=False, key=None):
        self.eng = eng
        self.fn = fn
        self.deps = [d for d in deps if d is not None]
        self.sig = False
        self.seq = None
        self.key = key
        self.kval = None
        self.is_dma = is_dma


class Sched:
    def __init__(self, nc):
        self.nc = nc
        self.ops = {e: [] for e in ENGS}
        self.keycount = {}

    def add(self, eng, fn, deps=()):
        op = Op(eng, fn, deps)
        self.ops[eng].append(op)
        return op

    def dma(self, queue, fn, key, deps=()):
        op = Op(queue, fn, deps, is_dma=True, key=key)
        self.keycount[key] = self.keycount.get(key, 0) + 1
        op.kval = 16 * self.keycount[key]
        self.ops[queue].append(op)
        return op

    def emit(self):
        nc = self.nc
        for e in ENGS:
            for op in self.ops[e]:
                for d in op.deps:
                    if d.is_dma:
                        continue
                    if d.eng == "pe" and op.eng == "pe" and not op.is_dma:
                        continue
                    d.sig = True
        nsig = {}
        for e in ENGS:
            n = 0
            for op in self.ops[e]:
                if op.sig and not op.is_dma:
                    op.seq = n
                    n += 1
            nsig[e] = n
        sems = {}
        for e in ENGS:
            for ep in range((nsig[e] + EPOCH - 1) // EPOCH):
                sems[(e, ep)] = nc.alloc_semaphore(f"c_{e}_{ep}")
        ksems = {k: nc.alloc_semaphore(f"k_{k}") for k in self.keycount}
        engobj = {"pe": nc.tensor, "act": nc.scalar, "dve": nc.vector, "pool": nc.gpsimd, "sp": nc.sync}

        def run_engine(e):
            eng = engobj[e]
            waited = {}
            for op in self.ops[e]:
                need = {}
                for d in op.deps:
                    if d.is_dma:
                        s = ksems[d.key]
                        v = d.kval
                    else:
                        if d.eng == "pe" and e == "pe" and not op.is_dma:
                            continue
                        s = sems[(d.eng, d.seq // EPOCH)]
                        v = d.seq % EPOCH + 1
                    if need.get(s.num, (None, 0))[1] < v:
                        need[s.num] = (s, v)
                for num, (s, v) in need.items():
                    if waited.get(num, 0) < v:
                        eng.wait_ge(s, v)
                        waited[num] = v
                ins = op.fn(eng)
                if op.is_dma:
                    ins.then_inc(ksems[op.key], 16)
                elif op.sig:
                    ins.then_inc(sems[(e, op.seq // EPOCH)], 1)

        with nc.Block() as block:
            @block.tensor
            def _(eng):
                run_engine("pe")

            @block.scalar
            def _(eng):
                run_engine("act")

            @block.vector
            def _(eng):
                run_engine("dve")

            @block.gpsimd
            def _(eng):
                run_engine("pool")

            @block.sync
            def _(eng):
                run_engine("sp")


class Buf:
    def __init__(self):
        self.w = {}
        self.r = {}
        self.prev = []

    @staticmethod
    def _put(d, op):
        d[id(op) if op.is_dma else op.eng] = op

    def rd(self):
        return list(self.w.values())

    def wr(self):
        return list(self.w.values()) + list(self.r.values())

    def did_read(self, op):
        self._put(self.r, op)

    def did_write(self, op):
        self.w = {}
        self.r = {}
        self._put(self.w, op)

    def start_gen(self):
        self.prev = self.wr()
        self.w = {}
        self.r = {}
        return self.prev

    def also_wrote(self, op):
        self._put(self.w, op)


def build_nc(dbg=False, upto="all"):
    nc = bass.Bass("TRN2", target_bir_lowering=False)
    S = Sched(nc)

    def din(name, shape):
        return nc.dram_tensor(name, list(shape), F32, kind="ExternalInput")

    x_d = din("x", (SEQ, D))
    meta_d = din("meta_tokens", (NMETA, D))
    nmix_d = din("norm_mix", (2, D))
    nffn_d = din("norm_ffn", (2, D))
    are_d = din("s5_a_re", (G, 64))
    aim_d = din("s5_a_im", (G, 64))
    ldt_d = din("s5_log_dt", (1, G))
    bre_d = din("s5_b_re", (G, 64, 16))
    bim_d = din("s5_b_im", (G, 64, 16))
    cre_d = din("s5_c_re", (G * 16, 64))
    cim_d = din("s5_c_im", (G * 16, 64))
    sd_d = din("s5_d", (1, D))
    wglu_d = din("s5_w_glu", (D, 2 * D))
    nkv_d = din("norm_kv", (1, D))
    wkv_d = din("w_kv", (D, 2 * D))
    wq_d = din("w_q", (D, D))
    wo_d = din("w_o", (D, D))
    win_d = din("w_ffn_in", (2, D, 2 * DFF))
    wout_d = din("w_ffn_out", (2, DFF, D))
    nfin_d = din("norm_final", (1, D))
    out_d = nc.dram_tensor("out", [SEQ, D], F32, kind="ExternalOutput")
    dbg_d = nc.dram_tensor("dbg", [L, D], F32, kind="ExternalOutput") if dbg else None
    dbgW = nc.dram_tensor("dbgW", [128, 2 * G * 64], F32, kind="ExternalOutput") if dbg else None
    dbgW0 = nc.dram_tensor("dbgW0", [128, 2 * G * 64], F32, kind="ExternalOutput") if dbg else None
    dbgU = nc.dram_tensor("dbgU", [128, G * 128], BF16, kind="ExternalOutput") if dbg else None
    dbgcnt = [0]

    KT_all = nc.dram_tensor("kt_all", [D, L], BF16, kind="Internal")
    V_all = nc.dram_tensor("v_all", [L, D], BF16, kind="Internal")
    s5scr = [nc.dram_tensor(f"s5scr{i}", [128, G * 128], BF16, kind="Internal") for i in range(4)]

    SB0 = 16512
    SBEND = 229344
    cur = [SB0]

    def alloc(name, shape, dt, at=None):
        nbytes = int(np.prod(shape[1:])) * (4 if dt in (F32, I32) else 2)
        nbytes = (nbytes + 31) // 32 * 32
        if at is None:
            off = cur[0]
            cur[0] += nbytes
            assert cur[0] <= SBEND, (name, cur[0])
        else:
            off = at
            assert off + nbytes <= SBEND, (name, off + nbytes)
        return nc.alloc_sbuf_tensor_at(name, list(shape), dt, offset=off), off + nbytes

    ident, _ = alloc("ident", [128, 128], BF16)
    identf, _ = alloc("identf", [128, 128], F32)
    onesf, _ = alloc("onesf", [128, 128], F32)
    TU, _ = alloc("TU", [128, 128], BF16)
    TL, _ = alloc("TL", [128, 128], BF16)
    dmask, _ = alloc("dmask", [128, 4, 512], BF16)
    gcol, _ = alloc("gcol", [128, 5, 8], F32)
    gfin, _ = alloc("gfin", [128, D], F32)
    gcolS5, _ = alloc("gcolS5", [128, G], F32)
    A1, _ = alloc("A1", [128, 2, G], F32)
    A2, _ = alloc("A2", [128, 2, G], F32)
    Zc, _ = alloc("Zc", [128, 2, G], F32)
    sT1, _ = alloc("sT1", [128, 2, G], F32)
    sT2, _ = alloc("sT2", [128, 2, G], F32)
    ssq, _ = alloc("ssq", [128, 8], F32)
    rstd, _ = alloc("rstd", [128, 8], F32)
    ring, _ = alloc("ring", [128, 4, 4096], BF16)
    h, _ = alloc("h", [128, 8, D], F32)
    ARENA = cur[0] - 32768
    hs, _ = alloc("hs", [128, 8 * D], BF16)
    xnT, _ = alloc("xnT", [128, FC, 1024], BF16)
    OV = cur[0]

    U, e1 = alloc("U", [128, G, 128], BF16, at=OV)
    Wbuf, e2 = alloc("Wbuf", [128, 64, 2, G], F32, at=e1)
    Xb, e3 = alloc("Xb", [128, G, 128], BF16, at=e2)
    sgb, _ = alloc("sgb", [128, 8, 512], F32, at=e1)
    outs, _ = alloc("outs", [128, 8, D], F32, at=OV)
    act, f1 = alloc("act", [128, MF, 1024], BF16, at=OV)
    woutb, f2 = alloc("woutb", [128, MF, 1024], BF16, at=f1)
    silu_t, f3 = alloc("silu_t", [128, 2, 512], F32, at=f2)
    Kst, k1 = alloc("Kst", [128, 8, 1024], BF16, at=OV)
    Vst, k2 = alloc("Vst", [128, 8, 1024], BF16, at=k1)
    QT, a1 = alloc("QT", [128, 8, 1024], BF16, at=OV)
    oT, a2 = alloc("oT", [128, 8, 1024], BF16, at=a1)
    KTp, a3 = alloc("KTp", [128, 2, 4128], BF16, at=a2)
    Vp, a4 = alloc("Vp", [128, 2, 33 * 128], BF16, at=a3)
    xnT2, _ = alloc("xnT2", [128, FC, 1024], BF16, at=a4)
    Et, a5 = alloc("Et", [128, 2, 1024], F32, at=a4)
    SPt, a6 = alloc("SPt", [128, 2, 1024], F32, at=a5)
    ARGt, a7 = alloc("ARGt", [128, 2, 1024], F32, at=a6)
    LMt, a8 = alloc("LMt", [128, 2, 1024], BF16, at=a7)
    Wtt, a9 = alloc("Wtt", [128, 2, 1024], BF16, at=a8)

    PBt = nc.alloc_psum_tensor("pball", [128, 8 * 512], F32)

    class Bank:
        def __init__(self, idx):
            self.idx = idx

        def __getitem__(self, key):
            if not isinstance(key, tuple):
                key = (key, slice(0, 512))
            rows, cols = key
            c0 = 0 if cols.start is None else cols.start
            c1 = 512 if cols.stop is None else cols.stop
            return PBt[rows, 512 * self.idx + c0:512 * self.idx + c1]

        def ap(self, off, dims, parts=128):
            return bass.AP(PBt, 512 * self.idx + off, [[4096, parts]] + [list(d) for d in dims])

    PB = [Bank(i) for i in range(8)]
    pbuf = [Buf() for _ in range(8)]
    rot = [0]

    rotmod = [4]

    def next_bank():
        b = rot[0] % rotmod[0]
        rot[0] = (b + 1) % rotmod[0]
        return b

    def sap(t, off, dims, parts=128, p0=0):
        pstep = int(np.prod(t.shape[1:]))
        return bass.AP(t, p0 * pstep + off, [[pstep, parts]] + [list(d) for d in dims])

    B = {n: Buf() for n in ["h", "hs", "xnT", "ssq", "rstd", "U", "Wbuf", "Xb", "Zc", "sT1", "sT2", "sgb", "act", "woutb",
                            "Kst", "Vst", "QT", "oT", "const", "silu0", "silu1", "xnT2", "outs"]}
    ringB = [Buf() for _ in range(4)]
    ringi = [0]

    def OP(eng, fn, reads=(), writes=(), partial=(), extra=()):
        deps = list(extra)
        for b in reads:
            deps += b.rd()
        for b in writes:
            deps += b.wr()
        for b in partial:
            deps += b.prev
        op = S.add(eng, fn, deps)
        for b in reads:
            b.did_read(op)
        for b in writes:
            b.did_write(op)
        for b in partial:
            b.also_wrote(op)
        return op

    def DMA(queue, fn, key, reads=(), writes=(), partial=(), extra=()):
        deps = list(extra)
        for b in reads:
            deps += b.rd()
        for b in writes:
            deps += b.wr()
        for b in partial:
            deps += b.prev
        op = S.dma(queue, fn, key, deps)
        for b in reads:
            b.did_read(op)
        for b in writes:
            b.did_write(op)
        for b in partial:
            b.also_wrote(op)
        return op

    CONST = B["const"]

    OP("pool", lambda e: e.memset(onesf[:], 1.0), writes=[Buf()])
    o_ones = S.ops["pool"][-1]
    c_ops = []
    c_ops.append(S.add("pool", lambda e: e.affine_select(out=identf[:], in_=onesf[:], pattern=[[-1, 128]], compare_op=ALU.is_equal,
                                                         fill=0.0, base=0, channel_multiplier=1), [o_ones]))
    c_ops.append(S.add("pool", lambda e: e.affine_select(out=ident[:], in_=onesf[:], pattern=[[-1, 128]], compare_op=ALU.is_equal,
                                                         fill=0.0, base=0, channel_multiplier=1), [o_ones]))
    c_ops.append(S.add("pool", lambda e: e.affine_select(out=TU[:], in_=onesf[:], pattern=[[-1, 128]], compare_op=ALU.is_gt,
                                                         fill=0.0, base=0, channel_multiplier=1), [o_ones]))
    c_ops.append(S.add("pool", lambda e: e.affine_select(out=TL[:], in_=onesf[:], pattern=[[1, 128]], compare_op=ALU.is_ge,
                                                         fill=0.0, base=0, channel_multiplier=-1), [o_ones]))
    for d in range(4):
        c_ops.append(S.add("pool", lambda e, d=d: e.affine_select(out=dmask[:, d, :], in_=sap(onesf, 0, [[0, 4], [1, 128]]),
                                                                  pattern=[[1, 512]], compare_op=ALU.is_gt, fill=0.0,
                                                                  base=(16 if d == 1 else -128 * d), channel_multiplier=-1), [o_ones]))
    with nc.allow_non_contiguous_dma(reason="tiny one-time parameter layouts"):
        pass
    gsrcs = [(nmix_d, 0), (nffn_d, 0), (nkv_d, 0), (nmix_d, 1), (nffn_d, 1)]

    def ncdma(e, out, in_):
        return e.dma_start(out=out, in_=in_, allow_slow_non_contiguous=True)

    for i, (src, row) in enumerate(gsrcs):
        c_ops.append(S.dma("act", lambda e, i=i, src=src, row=row: ncdma(
            e, gcol[:, i, :], bass.AP(src, row * D, [[1, 128], [128, 8]])), "cst"))
    c_ops.append(S.dma("act", lambda e: e.dma_start(out=gfin[:], in_=bass.AP(nfin_d, 0, [[0, 128], [1, D]])), "cst"))
    for j in range(8):
        c_ops.append(S.dma("act", lambda e, j=j: ncdma(e, gcolS5[16 * j:16 * j + 16, :], bass.AP(nmix_d, 0, [[1, 16], [16, G]])), "cst"))
    for op in c_ops:
        CONST.also_wrote(op)

    scur = [ARENA]

    def salloc(name, shape, dt):
        t, e = alloc(name, shape, dt, at=scur[0])
        scur[0] = e
        return t

    ARE = salloc("ARE", [128, G], F32)
    AIM = salloc("AIM", [128, G], F32)
    DT = salloc("DT", [128, G], F32)
    X1 = salloc("X1", [128, G], F32)
    ANG = salloc("ANG", [128, G], F32)
    TMPA = salloc("TMPA", [128, G], F32)
    TMPB = salloc("TMPB", [128, G], F32)
    TMPI = salloc("TMPI", [128, G], I32)
    MAG = salloc("MAG", [128, G], F32)
    IMG2 = salloc("IMG2", [128, G], F32)
    COSA = salloc("COSA", [128, G], F32)
    SINA = salloc("SINA", [128, G], F32)
    PWR = salloc("PWR", [128, 16, G], F32)
    PWI = salloc("PWI", [128, 16, G], F32)
    CFR = salloc("CFR", [128, G], F32)
    CFI = salloc("CFI", [128, G], F32)
    SGN = salloc("SGN", [128, 1], F32)
    DCOL = salloc("DCOL", [128, G], F32)
    BRE = salloc("BRE", [128, G, 16], F32)
    BIM = salloc("BIM", [128, G, 16], F32)
    BB1 = salloc("BB1", [128, G, 16], F32)
    BB2 = salloc("BB2", [128, G, 16], F32)
    CC1 = salloc("CC1", [128, G, 16], F32)
    CC2 = salloc("CC2", [128, G, 16], F32)
    CNR = salloc("CNR", [128, 8, 128], F32)
    CNI = salloc("CNI", [128, 8, 128], F32)
    TM1 = salloc("TM1", [128, G * 16], F32)
    TM2 = salloc("TM2", [128, G * 16], F32)
    BLK = salloc("BLK", [128, 128], F32)
    TMM = salloc("TMM", [128, 512], F32)
    PA = salloc("PA", [128, G, 128], BF16)
    QAm = salloc("QAm", [128, G, 128], BF16)
    QAo = salloc("QAo", [128, G, 128], BF16)
    Mm = salloc("Mm", [128, G, 128], BF16)
    P1 = salloc("P1", [128, G, 128], BF16)
    P2 = salloc("P2", [128, G, 128], BF16)

    sb = {}

    def T(name):
        if name not in sb:
            sb[name] = Buf()
        return sb[name]

    def dve(fn, reads, writes):
        return OP("dve", fn, reads=[T(r) for r in reads], writes=[T(w) for w in writes])

    def actop(fn, reads, writes):
        return OP("act", fn, reads=[T(r) for r in reads], writes=[T(w) for w in writes])

    for half in range(2):
        ps_ = slice(64 * half, 64 * half + 64)
        DMA("sp", lambda e, ps_=ps_: ncdma(e, ARE[ps_, :], bass.AP(are_d, 0, [[1, 64], [64, G]])), "s5l_ARE", partial=[T("ARE")])
        DMA("sp", lambda e, ps_=ps_: ncdma(e, AIM[ps_, :], bass.AP(aim_d, 0, [[1, 64], [64, G]])), "s5l_AIM", partial=[T("AIM")])
        DMA("sp", lambda e, ps_=ps_: ncdma(e, BRE[ps_, :, :], bass.AP(bre_d, 0, [[16, 64], [1024, G], [1, 16]])), "s5l_BRE", partial=[T("BRE")])
        DMA("sp", lambda e, ps_=ps_: ncdma(e, BIM[ps_, :, :], bass.AP(bim_d, 0, [[16, 64], [1024, G], [1, 16]])), "s5l_BIM", partial=[T("BIM")])
        DMA("sp", lambda e, half=half: ncdma(e, CNR[:, :, 64 * half:64 * half + 64], bass.AP(cre_d, 0, [[64, 128], [8192, 8], [1, 64]])),
            "s5l_CNR", partial=[T("CNR")])
        DMA("sp", lambda e, half=half: ncdma(e, CNI[:, :, 64 * half:64 * half + 64], bass.AP(cim_d, 0, [[64, 128], [8192, 8], [1, 64]])),
            "s5l_CNI", partial=[T("CNI")])
    DMA("sp", lambda e: e.dma_start(out=DT[:], in_=bass.AP(ldt_d, 0, [[0, 128], [1, G]])), "s5l_DT", writes=[T("DT")])
    for j in range(8):
        DMA("sp", lambda e, j=j: ncdma(e, DCOL[16 * j:16 * j + 16, :], bass.AP(sd_d, 0, [[1, 16], [16, G]])), "s5l_DCOL", partial=[T("DCOL")])
    OP("pool", lambda e: e.memset(SGN[0:64, :], -1.0), partial=[T("SGN")])
    OP("pool", lambda e: e.memset(SGN[64:128, :], 1.0), partial=[T("SGN")])
    OP("pool", lambda e: e.memset(Zc[:], 0.0), writes=[B["Zc"]])
    for i in range(8):
        OP("pool", lambda e, i=i: e.affine_select(out=BLK[:, 16 * i:16 * i + 16], in_=onesf[:, 0:16], pattern=[[0, 16]], compare_op=ALU.is_gt,
                                                  fill=0.0, base=16 * (i + 1), channel_multiplier=-1), reads=[CONST], partial=[T("BLK")])

    actop(lambda e: e.activation(out=DT[:], in_=DT[:], func=AF.Exp), ["DT"], ["DT"])
    dve(lambda e: e.tensor_tensor(out=X1[:], in0=DT[:], in1=ARE[:], op=ALU.mult), ["DT", "ARE"], ["X1"])
    dve(lambda e: e.tensor_tensor(out=ANG[:], in0=DT[:], in1=AIM[:], op=ALU.mult), ["DT", "AIM"], ["ANG"])
    dve(lambda e: e.tensor_scalar(out=TMPA[:], in0=X1[:], scalar1=1.0 / 720.0, scalar2=None, op0=ALU.mult), ["X1"], ["TMPA"])
    for cst in (1.0 / 120.0, 1.0 / 24.0, 1.0 / 6.0, 0.5, 1.0):
        dve(lambda e, cst=cst: e.scalar_tensor_tensor(out=TMPA[:], in0=TMPA[:], scalar=cst, in1=X1[:], op0=ALU.add, op1=ALU.mult),
            ["TMPA", "X1"], ["TMPA"])
    dve(lambda e: e.tensor_scalar(out=MAG[:], in0=TMPA[:], scalar1=1.0, scalar2=None, op0=ALU.add), ["TMPA"], ["MAG"])
    dve(lambda e: e.tensor_scalar(out=TMPB[:], in0=X1[:], scalar1=-2.0, scalar2=None, op0=ALU.mult), ["X1"], ["TMPB"])
    dve(lambda e: e.tensor_scalar(out=TMPA[:], in0=TMPB[:], scalar1=1.0 / 720.0, scalar2=None, op0=ALU.mult), ["TMPB"], ["TMPA"])
    for cst in (1.0 / 120.0, 1.0 / 24.0, 1.0 / 6.0, 0.5, 1.0):
        dve(lambda e, cst=cst: e.scalar_tensor_tensor(out=TMPA[:], in0=TMPA[:], scalar=cst, in1=TMPB[:], op0=ALU.add, op1=ALU.mult),
            ["TMPA", "TMPB"], ["TMPA"])
    dve(lambda e: e.tensor_scalar(out=IMG2[:], in0=TMPA[:], scalar1=1.0, scalar2=None, op0=ALU.add), ["TMPA"], ["IMG2"])

    def trig(dst, dname, shift):
        dve(lambda e: e.tensor_scalar(out=TMPA[:], in0=ANG[:], scalar1=shift, scalar2=1.0 / (2 * math.pi), op0=ALU.add, op1=ALU.mult),
            ["ANG"], ["TMPA"])
        dve(lambda e: e.tensor_copy(out=TMPI[:], in_=TMPA[:]), ["TMPA"], ["TMPI"])
        dve(lambda e: e.tensor_copy(out=TMPB[:], in_=TMPI[:]), ["TMPI"], ["TMPB"])
        dve(lambda e: e.scalar_tensor_tensor(out=TMPA[:], in0=TMPB[:], scalar=-2 * math.pi, in1=ANG[:], op0=ALU.mult, op1=ALU.add),
            ["TMPB", "ANG"], ["TMPA"])
        dve(lambda e: e.tensor_scalar(out=TMPA[:], in0=TMPA[:], scalar1=shift, scalar2=math.pi, op0=ALU.add, op1=ALU.min),
            ["TMPA"], ["TMPA"])
        dve(lambda e: e.tensor_scalar(out=TMPA[:], in0=TMPA[:], scalar1=-math.pi, scalar2=None, op0=ALU.max), ["TMPA"], ["TMPA"])
        actop(lambda e: e.activation(out=dst[:], in_=TMPA[:], func=AF.Sin), ["TMPA"], [dname])

    sb["SINA"] = Buf()
    sb["COSA"] = Buf()
    trig(SINA, 'SINA', 0.0)
    trig(COSA, 'COSA', math.pi / 2)
    T7 = 7
    OP("pool", lambda e: e.memset(PWR[:, T7, :], 1.0), partial=[T("PWR")])
    OP("pool", lambda e: e.memset(PWI[:, T7, :], 0.0), partial=[T("PWI")])
    dve(lambda e: e.tensor_tensor(out=PWR[:, T7 + 1, :], in0=MAG[:], in1=COSA[:], op=ALU.mult), ["MAG", "COSA"], [])
    o = S.ops["dve"][-1]
    T("PWR").also_wrote(o)
    dve(lambda e: e.tensor_tensor(out=PWI[:, T7 + 1, :], in0=MAG[:], in1=SINA[:], op=ALU.mult), ["MAG", "SINA"], [])
    o = S.ops["dve"][-1]
    T("PWI").also_wrote(o)
    dve(lambda e: e.tensor_tensor(out=PWR[:, T7 - 1, :], in0=PWR[:, T7 + 1, :], in1=IMG2[:], op=ALU.mult), ["PWR", "IMG2"], [])
    T("PWR").also_wrote(S.ops["dve"][-1])
    dve(lambda e: e.scalar_tensor_tensor(out=PWI[:, T7 - 1, :], in0=PWI[:, T7 + 1, :], scalar=-1.0, in1=IMG2[:], op0=ALU.mult, op1=ALU.mult),
        ["PWI", "IMG2"], [])
    T("PWI").also_wrote(S.ops["dve"][-1])

    def cmul(dst, a, b):
        dve(lambda e: e.tensor_tensor(out=TMPA[:], in0=PWR[:, a, :], in1=PWR[:, b, :], op=ALU.mult), ["PWR"], ["TMPA"])
        dve(lambda e: e.tensor_tensor(out=TMPB[:], in0=PWI[:, a, :], in1=PWI[:, b, :], op=ALU.mult), ["PWI"], ["TMPB"])
        dve(lambda e: e.tensor_tensor(out=PWR[:, dst, :], in0=TMPA[:], in1=TMPB[:], op=ALU.subtract), ["TMPA", "TMPB", "PWR"], [])
        o1 = S.ops["dve"][-1]
        dve(lambda e: e.tensor_tensor(out=TMPA[:], in0=PWR[:, a, :], in1=PWI[:, b, :], op=ALU.mult), ["PWR", "PWI"], ["TMPA"])
        dve(lambda e: e.tensor_tensor(out=TMPB[:], in0=PWI[:, a, :], in1=PWR[:, b, :], op=ALU.mult), ["PWR", "PWI"], ["TMPB"])
        dve(lambda e: e.tensor_tensor(out=PWI[:, dst, :], in0=TMPA[:], in1=TMPB[:], op=ALU.add), ["TMPA", "TMPB", "PWI"], [])
        o2 = S.ops["dve"][-1]
        T("PWR").also_wrote(o1)
        T("PWI").also_wrote(o2)

    for tau in range(2, 9):
        cmul(T7 + tau, T7 + tau - 1, T7 + 1)
    for tau in range(2, 8):
        cmul(T7 - tau, T7 - tau + 1, T7 - 1)

    dve(lambda e: e.tensor_tensor(out=TMPA[:], in0=ARE[:], in1=ARE[:], op=ALU.mult), ["ARE"], ["TMPA"])
    dve(lambda e: e.tensor_tensor(out=TMPB[:], in0=AIM[:], in1=AIM[:], op=ALU.mult), ["AIM"], ["TMPB"])
    dve(lambda e: e.tensor_tensor(out=TMPA[:], in0=TMPA[:], in1=TMPB[:], op=ALU.add), ["TMPA", "TMPB"], ["TMPA"])
    dve(lambda e: e.reciprocal(out=IMG2[:], in_=TMPA[:]), ["TMPA"], ["IMG2"])
    dve(lambda e: e.tensor_scalar(out=MAG[:], in0=PWR[:, T7 + 1, :], scalar1=-1.0, scalar2=None, op0=ALU.add), ["PWR"], ["MAG"])
    dve(lambda e: e.tensor_tensor(out=TMPA[:], in0=MAG[:], in1=ARE[:], op=ALU.mult), ["MAG", "ARE"], ["TMPA"])
    dve(lambda e: e.tensor_tensor(out=TMPB[:], in0=PWI[:, T7 + 1, :], in1=AIM[:], op=ALU.mult), ["PWI", "AIM"], ["TMPB"])
    dve(lambda e: e.tensor_tensor(out=TMPA[:], in0=TMPA[:], in1=TMPB[:], op=ALU.add), ["TMPA", "TMPB"], ["TMPA"])
    dve(lambda e: e.tensor_tensor(out=CFR[:], in0=TMPA[:], in1=IMG2[:], op=ALU.mult), ["TMPA", "IMG2"], ["CFR"])
    dve(lambda e: e.tensor_tensor(out=TMPA[:], in0=PWI[:, T7 + 1, :], in1=ARE[:], op=ALU.mult), ["PWI", "ARE"], ["TMPA"])
    dve(lambda e: e.tensor_tensor(out=TMPB[:], in0=MAG[:], in1=AIM[:], op=ALU.mult), ["MAG", "AIM"], ["TMPB"])
    dve(lambda e: e.tensor_tensor(out=TMPA[:], in0=TMPA[:], in1=TMPB[:], op=ALU.subtract), ["TMPA", "TMPB"], ["TMPA"])
    dve(lambda e: e.tensor_tensor(out=CFI[:], in0=TMPA[:], in1=IMG2[:], op=ALU.mult), ["TMPA", "IMG2"], ["CFI"])

    def bc16(t, idx=None):
        if idx is None:
            return sap(t, 0, [[1, G], [0, 16]])
        return sap(t, idx * G, [[1, G], [0, 16]])

    def halves(fn_top, fn_bot):
        fn_top(slice(0, 64))
        fn_bot(slice(64, 128))

    TM1v = TM1[:].rearrange("p (g c) -> p g c", c=16)
    TM2v = TM2[:].rearrange("p (g c) -> p g c", c=16)
    dve(lambda e: e.tensor_tensor(out=TM1v, in0=BRE[:], in1=bc16(CFR), op=ALU.mult), ["BRE", "CFR"], ["TM1"])
    dve(lambda e: e.tensor_tensor(out=TM2v, in0=BIM[:], in1=bc16(CFI), op=ALU.mult), ["BIM", "CFI"], ["TM2"])
    dve(lambda e: e.tensor_tensor(out=TM1v, in0=TM1v, in1=TM2v, op=ALU.subtract), ["TM1", "TM2"], ["TM1"])
    dve(lambda e: e.tensor_copy(out=BB1[0:64], in_=TM1v[0:64]), ["TM1"], [])
    T("BB1").also_wrote(S.ops["dve"][-1])
    dve(lambda e: e.tensor_copy(out=BB2[64:128], in_=TM1v[64:128]), ["TM1"], [])
    T("BB2").also_wrote(S.ops["dve"][-1])
    dve(lambda e: e.tensor_tensor(out=TM1v, in0=BIM[:], in1=bc16(CFR), op=ALU.mult), ["BIM", "CFR", "TM1"], ["TM1"])
    dve(lambda e: e.tensor_tensor(out=TM2v, in0=BRE[:], in1=bc16(CFI), op=ALU.mult), ["BRE", "CFI"], ["TM2"])
    dve(lambda e: e.tensor_tensor(out=TM1v, in0=TM1v, in1=TM2v, op=ALU.add), ["TM1", "TM2"], ["TM1"])
    dve(lambda e: e.tensor_copy(out=BB1[64:128], in_=TM1v[64:128]), ["TM1"], [])
    T("BB1").also_wrote(S.ops["dve"][-1])
    dve(lambda e: e.tensor_scalar(out=BB2[0:64], in0=TM1v[0:64], scalar1=-1.0, scalar2=None, op0=ALU.mult), ["TM1"], [])
    T("BB2").also_wrote(S.ops["dve"][-1])

    for which, CN, cname in ((0, CNR, 'CNR'), (1, CNI, 'CNI')):
        for blk in range(8):
            b = next_bank()
            pbuf[b].start_gen()
            OP("pe", lambda e, b=b, CN=CN, blk=blk: e.matmul(PB[b][:, 0:128], lhsT=CN[:, blk, :], rhs=identf[:], start=True, stop=True),
               reads=[T(cname), CONST], partial=[pbuf[b]])
            gsl = slice(8 * blk, 8 * blk + 8)
            pv = PB[b][:, 0:128].rearrange("p (g c) -> p g c", c=16)
            if which == 0:
                o_ = OP("act", lambda e, pv=pv, gsl=gsl: e.activation(out=CC1[0:64, gsl, :], in_=pv[0:64], func=AF.Copy),
                        reads=[pbuf[b]], partial=[T("CC1")])
                OP("dve", lambda e, pv=pv, gsl=gsl: e.tensor_scalar(out=CC2[64:128, gsl, :], in0=pv[64:128], scalar1=-1.0, scalar2=None, op0=ALU.mult),
                   reads=[pbuf[b]], partial=[T("CC2")], extra=[o_])
            else:
                o_ = OP("act", lambda e, pv=pv, gsl=gsl: e.activation(out=CC1[64:128, gsl, :], in_=pv[64:128], func=AF.Copy, scale=-1.0),
                        reads=[pbuf[b]], partial=[T("CC1")])
                OP("dve", lambda e, pv=pv, gsl=gsl: e.tensor_scalar(out=CC2[0:64, gsl, :], in0=pv[0:64], scalar1=-1.0, scalar2=None, op0=ALU.mult),
                   reads=[pbuf[b]], partial=[T("CC2")], extra=[o_])

    def combo(dst, dname, slot, tidx, X1t, X2t, x1n, x2n):
        dv = sap(dst, slot * 16, [[128, G], [1, 16]])
        dve(lambda e: e.tensor_tensor(out=TM1v, in0=X1t[:], in1=bc16(PWR, tidx), op=ALU.mult), [x1n, "PWR", "TM1"], ["TM1"])
        OP("dve", lambda e: e.tensor_tensor(out=TM2v, in0=X2t[:], in1=bc16(PWI, tidx), op=ALU.mult),
           reads=[T(x2n), T("PWI")], writes=[T("TM2")])
        dve(lambda e: e.tensor_tensor(out=dv, in0=TM1v, in1=TM2v, op=ALU.add), ["TM1", "TM2"], [])
        T(dname).also_wrote(S.ops["dve"][-1])

    for j in range(8):
        combo(PA, 'PA', j, T7 + (7 - j), BB1, BB2, "BB1", "BB2")
    for i in range(8):
        combo(QAm, 'QAm', i, T7 + (i - 7), CC1, CC2, "CC1", "CC2")
    for i in range(8):
        combo(QAo, 'QAo', i, T7 + (i + 1), CC1, CC2, "CC1", "CC2")

    for g0 in range(0, G, 4):
        b = next_bank()
        pbuf[b].start_gen()
        for gg in range(4):
            OP("pe", lambda e, b=b, g=g0 + gg, gg=gg: e.matmul(PB[b][:, 128 * gg:128 * gg + 128], lhsT=PA[:, g, :], rhs=QAm[:, g, :],
                                                             start=True, stop=True), reads=[T("PA"), T("QAm")], partial=[pbuf[b]])
        OP("dve", lambda e, b=b: e.tensor_tensor(out=TMM[:].rearrange("p (g n) -> p g n", g=4), in0=PB[b][:].rearrange("p (g n) -> p g n", g=4),
                                                in1=sap(BLK, 0, [[0, 4], [1, 128]]), op=ALU.mult),
           reads=[pbuf[b], T("BLK")], writes=[T("TMM")])
        for gg in range(4):
            OP("dve", lambda e, g=g0 + gg, gg=gg: e.scalar_tensor_tensor(out=Mm[:, g, :], in0=identf[:], scalar=DCOL[:, g:g + 1],
                                                                       in1=TMM[:, 128 * gg:128 * gg + 128], op0=ALU.mult, op1=ALU.add),
               reads=[T("TMM"), T("DCOL"), CONST], partial=[T("Mm")])
        b2 = next_bank()
        pbuf[b2].start_gen()
        for gg in range(4):
            OP("pe", lambda e, b2=b2, g=g0 + gg, gg=gg: e.matmul(PB[b2][:, 128 * gg:128 * gg + 128], lhsT=PA[:, g, :], rhs=ident[:],
                                                               start=True, stop=True), reads=[T("PA"), CONST], partial=[pbuf[b2]])
        pv = PB[b2][:].rearrange("p (g n) -> p g n", g=4)
        o1_ = OP("act", lambda e, pv=pv, g0=g0: e.activation(out=P1[:, g0:g0 + 4, :], in_=pv, func=AF.Copy), reads=[pbuf[b2]], partial=[T("P1")])
        o2_ = OP("act", lambda e, pv=pv, g0=g0: e.activation(out=P2[:, g0:g0 + 4, 64:128], in_=pv[:, :, 0:64], func=AF.Copy), reads=[pbuf[b2]],
                 partial=[T("P2")])
        OP("dve", lambda e, pv=pv, g0=g0: e.tensor_copy(out=P2[:, g0:g0 + 4, 0:64], in_=pv[:, :, 64:128]), reads=[pbuf[b2]], partial=[T("P2")],
           extra=[o1_, o2_])
    T8 = T7 + 8
    OP("dve", lambda e: e.tensor_copy(out=A1[:, 0, :], in_=PWR[:, T8, :]), reads=[T("PWR")], partial=[CONST])
    OP("dve", lambda e: e.tensor_copy(out=A1[:, 1, :], in_=PWR[:, T8, :]), reads=[T("PWR")], partial=[CONST])
    OP("dve", lambda e: e.tensor_scalar(out=A2[:, 0, :], in0=PWI[:, T8, :], scalar1=SGN[:, 0:1], scalar2=None, op0=ALU.mult),
       reads=[T("PWI"), T("SGN")], partial=[CONST])
    OP("dve", lambda e: e.tensor_scalar(out=A2[:, 1, :], in0=PWI[:, T8, :], scalar1=SGN[:, 0:1], scalar2=-1.0, op0=ALU.mult, op1=ALU.mult),
       reads=[T("PWI"), T("SGN")], partial=[CONST])
    S5W = Buf()
    for i, (src, sname) in enumerate(((Mm, "Mm"), (P1, "P1"), (P2, "P2"), (QAo, "QAo"))):
        DMA("sp", lambda e, i=i, src=src: e.dma_start(out=s5scr[i].ap(), in_=src[:].rearrange("p g n -> p (g n)")), "s5st",
            reads=[T(sname)], partial=[S5W])
    setup_bufs = list(sb.values())

    def ring_load(srcs, extra=()):
        slot = ringi[0]
        ringi[0] = (ringi[0] + 1) % 4
        rb = ringB[slot]
        rb.start_gen()
        for (doff, ddims, src) in srcs:
            DMA("pool", lambda e, doff=doff, ddims=ddims, src=src, slot=slot: e.dma_start(out=sap(ring, slot * 4096 + doff, ddims), in_=src),
                f"ring{slot}", partial=[rb], extra=extra)
        return slot, rb

    def wsrc(dram, row0, col0, ncols, nrowchunks=8, rowlen=None):
        rl = rowlen if rowlen is not None else dram.shape[-1]
        return bass.AP(dram, row0 * rl + col0, [[rl, 128], [128 * rl, nrowchunks], [1, ncols]])

    presq = [False]
    HOOKS = [upto == "all"]

    def square_row(KT, j):
        if not HOOKS[0]:
            return
        if not presq[0]:
            presq[0] = True
            B["ssq"].start_gen()
            B["hs"].start_gen()
        OP("act", lambda e, j=j: e.activation(out=hs[0:KT, j * D:(j + 1) * D], in_=h[0:KT, j, :], func=AF.Square, accum_out=ssq[0:KT, j:j + 1]),
           reads=[B["h"]], partial=[B["ssq"], B["hs"]])

    def rmsnorm_to_hs(KT, perm):
        if presq[0]:
            presq[0] = False
        else:
            B["ssq"].start_gen()
            B["hs"].start_gen()
            for j in range(8):
                OP("act", lambda e, j=j: e.activation(out=hs[0:KT, j * D:(j + 1) * D], in_=h[0:KT, j, :], func=AF.Square, accum_out=ssq[0:KT, j:j + 1]),
                   reads=[B["h"]], partial=[B["ssq"], B["hs"]])
        OP("act", lambda e: e.activation(out=rstd[0:KT, :], in_=ssq[0:KT, :], func=AF.Ln, scale=1.0 / D, bias=EPS), reads=[B["ssq"]], writes=[B["rstd"]])
        OP("act", lambda e: e.activation(out=rstd[0:KT, :], in_=rstd[0:KT, :], func=AF.Exp, scale=-0.5), reads=[B["rstd"]], writes=[B["rstd"]])
        B["hs"].start_gen()
        for j in range(8):
            if perm:
                outap = sap(hs, j * 16, [[128, G], [1, 16]], parts=KT)
                inap = h[0:KT, j, :].rearrange("p (g c) -> p g c", c=16)
            else:
                outap = hs[0:KT, j * D:(j + 1) * D]
                inap = h[0:KT, j, :]
            if j % 2 == 0:
                OP("dve", lambda e, outap=outap, inap=inap, j=j: e.tensor_scalar(out=outap, in0=inap, scalar1=rstd[0:KT, j:j + 1], scalar2=None, op0=ALU.mult),
                   reads=[B["h"], B["rstd"]], partial=[B["hs"]])
            else:
                OP("act", lambda e, outap=outap, inap=inap, j=j: e.activation(out=outap, in_=inap, func=AF.Copy, scale=rstd[0:KT, j:j + 1]),
                   reads=[B["h"], B["rstd"]], partial=[B["hs"]])

    def transposes_to_fm(KT, gi, dst, dstB, gi2=None, dst2=None, dstB2=None):
        NT = 8 * KT
        dstB.start_gen()
        if dst2 is not None:
            dstB2.start_gen()
        cnt = 0
        for fc in range(FC):
            for jh in range(2):
                b = next_bank()
                pbuf[b].start_gen()
                for jj in range(4):
                    j = 4 * jh + jj
                    OP("pe", lambda e, b=b, j=j, jj=jj, fc=fc: e.matmul(PB[b][:, jj * KT:(jj + 1) * KT], lhsT=hs[0:KT, j * D + fc * 128:j * D + fc * 128 + 128],
                                                                      rhs=ident[0:KT, 0:KT], start=True, stop=True),
                       reads=[B["hs"], CONST], partial=[pbuf[b]])
                outap = sap(dst, fc * 1024 + 4 * jh, [[1, 4], [8, KT]])
                inap = PB[b].ap(0, [[KT, 4], [1, KT]], parts=128)
                if cnt % 2 == 0:
                    if gi is None:
                        OP("dve", lambda e, outap=outap, inap=inap: e.tensor_copy(out=outap, in_=inap), reads=[pbuf[b]], partial=[dstB])
                    else:
                        OP("dve", lambda e, outap=outap, inap=inap, fc=fc: e.tensor_scalar(out=outap, in0=inap, scalar1=gcol[:, gi, fc:fc + 1], scalar2=None,
                                                                                        op0=ALU.mult), reads=[pbuf[b], CONST], partial=[dstB])
                else:
                    if gi is None:
                        OP("act", lambda e, outap=outap, inap=inap: e.activation(out=outap, in_=inap, func=AF.Copy), reads=[pbuf[b]], partial=[dstB])
                    else:
                        OP("act", lambda e, outap=outap, inap=inap, fc=fc: e.activation(out=outap, in_=inap, func=AF.Copy, scale=gcol[:, gi, fc:fc + 1]),
                           reads=[pbuf[b], CONST], partial=[dstB])
                if dst2 is not None:
                    outap2 = sap(dst2, fc * 1024 + 4 * jh, [[1, 4], [8, KT]])
                    if cnt % 2 == 0:
                        OP("dve", lambda e, outap2=outap2, inap=inap, fc=fc: e.tensor_scalar(out=outap2, in0=inap, scalar1=gcol[:, gi2, fc:fc + 1], scalar2=None,
                                                                                          op0=ALU.mult), reads=[pbuf[b], CONST], partial=[dstB2])
                    else:
                        OP("act", lambda e, outap2=outap2, inap=inap, fc=fc: e.activation(out=outap2, in_=inap, func=AF.Copy, scale=gcol[:, gi2, fc:fc + 1]),
                           reads=[pbuf[b], CONST], partial=[dstB2])
                cnt += 1

    def alias_fence(new_bufs, old_bufs):
        deps = []
        for ob in old_bufs:
            deps += ob.wr()
        for nb in new_bufs:
            for d in deps:
                nb.r[("al", id(d))] = d

    def colgroups(NT):
        return [(c0, min(512, NT - c0)) for c0 in range(0, NT, 512)]

    def dbg_dump_h(KT, tok0):
        if dbg_d is None:
            return
        DMA("sp", lambda e: e.dma_start(out=bass.AP(dbg_d, tok0 * D, [[8 * D, KT], [1, 8 * D]]), in_=h[0:KT].rearrange("p j f -> p (j f)")),
            "dbg", reads=[B["h"]])

    def s5_layer(KT):
        NT = 8 * KT
        rmsnorm_to_hs(KT, perm=True)
        B["U"].start_gen()
        for g0 in range(0, G, 4):
            b = next_bank()
            pbuf[b].start_gen()
            for gg in range(4):
                g = g0 + gg
                OP("pe", lambda e, b=b, g=g, gg=gg: e.matmul(PB[b][:, gg * KT:(gg + 1) * KT], lhsT=hs[0:KT, g * 128:(g + 1) * 128], rhs=ident[0:KT, 0:KT],
                                                           start=True, stop=True), reads=[B["hs"], CONST], partial=[pbuf[b]])
            OP("dve", lambda e, b=b, g0=g0: e.tensor_tensor(out=U[:, g0:g0 + 4, 0:KT], in0=PB[b].ap(0, [[KT, 4], [1, KT]], parts=128),
                                                           in1=sap(gcolS5, g0, [[1, 4], [0, KT]]), op=ALU.mult),
               reads=[pbuf[b], CONST], partial=[B["U"]])
        segs = [(k0, min(64, KT - k0)) for k0 in range(0, KT, 64)]
        B["Xb"].start_gen()
        for (k0, kn) in segs:
            B["Wbuf"].start_gen()
            for lay in range(2):
                for gh in range(2):
                    slot, rb = ring_load([(0, [[1, 4096]], bass.AP(s5scr[1 + lay], gh * 4096, [[G * 128, 128], [1, 4096]]))], extra=S5W.rd())
                    for g8 in range(0, 32, 8):
                        b = next_bank()
                        pbuf[b].start_gen()
                        for gg in range(8):
                            gl = g8 + gg
                            g = gh * 32 + gl
                            OP("pe", lambda e, b=b, g=g, gl=gl, gg=gg, slot=slot, kn=kn, k0=k0: e.matmul(
                                PB[b][:, gg * kn:(gg + 1) * kn], lhsT=ring[:, slot, gl * 128:(gl + 1) * 128], rhs=U[:, g, k0:k0 + kn], start=True, stop=True),
                               reads=[rb, B["U"]], partial=[pbuf[b]])
                        gbase = gh * 32 + g8
                        eng = "act" if (g8 // 8) % 2 == 0 else "dve"
                        outap = sap(Wbuf, lay * G + gbase, [[1, 8], [2 * G, kn]])
                        inap = PB[b].ap(0, [[kn, 8], [1, kn]], parts=128)
                        if eng == "act":
                            OP("act", lambda e, outap=outap, inap=inap: e.activation(out=outap, in_=inap, func=AF.Copy), reads=[pbuf[b]], partial=[B["Wbuf"]])
                        else:
                            OP("dve", lambda e, outap=outap, inap=inap: e.tensor_copy(out=outap, in_=inap), reads=[pbuf[b]], partial=[B["Wbuf"]])
            if dbg_d is not None and KT == 128 and dbgcnt[0] == 0:
                DMA("sp", lambda e: e.dma_start(out=dbgW0.ap(), in_=Wbuf[:].rearrange("p a g k -> p (a g k)")), "dbgw", reads=[B["Wbuf"]])
                DMA("sp", lambda e: e.dma_start(out=dbgU.ap(), in_=U[:].rearrange("p g k -> p (g k)")), "dbgw", reads=[B["U"]])
            OP("act", lambda e, k0=k0: e.activation(out=sap(Xb, k0, [[128, G]]), in_=Zc[:, 0, :], func=AF.Copy), reads=[B["Zc"]], partial=[B["Xb"]])
            SCB = 4
            pA1 = PB[SCB].ap(0, [[G, 2], [1, G]])
            pA2 = PB[SCB].ap(128, [[G, 2], [1, G]])
            pT2 = PB[SCB].ap(256, [[G, 2], [1, G]])
            pS = PB[SCB].ap(384, [[G, 2], [1, G]])
            if k0 == 0:
                pbuf[SCB].start_gen()
                OP("dve", lambda e: e.tensor_copy(out=pA1, in_=A1[:]), reads=[CONST], partial=[pbuf[SCB]])
                OP("dve", lambda e: e.tensor_copy(out=pA2, in_=A2[:]), reads=[CONST], partial=[pbuf[SCB]])
            pcon = pbuf[SCB]
            for k in range(kn):
                if k == 0:
                    zfull = Zc[:]
                    zswap = sap(Zc, G, [[-G, 2], [1, G]])
                    zb = B["Zc"]
                else:
                    zfull = sap(Wbuf, (k - 1) * 2 * G, [[G, 2], [1, G]])
                    zswap = sap(Wbuf, (k - 1) * 2 * G + G, [[-G, 2], [1, G]])
                    zb = B["Wbuf"]
                wk = sap(Wbuf, k * 2 * G, [[G, 2], [1, G]])
                OP("dve", lambda e, zfull=zfull: e.tensor_tensor(out=sT1[:], in0=pA1, in1=zfull, op=ALU.mult), reads=[zb, pcon], writes=[B["sT1"]])
                OP("dve", lambda e, zswap=zswap: e.tensor_tensor(out=pT2, in0=pA2, in1=zswap, op=ALU.mult), reads=[zb, pcon], writes=[B["sT2"]])
                OP("dve", lambda e: e.tensor_tensor(out=pS, in0=pT2, in1=sT1[:], op=ALU.add), reads=[B["sT1"], B["sT2"]], writes=[B["sT2"]])
                o = OP("dve", lambda e, wk=wk: e.tensor_tensor(out=wk, in0=pS, in1=wk, op=ALU.add), reads=[B["sT2"], B["Wbuf"]], writes=[],
                       extra=list(B["Wbuf"].r.values()))
                B["Wbuf"].also_wrote(o)
            if kn > 1:
                OP("act", lambda e, k0=k0, kn=kn: e.activation(out=Xb[:, :, k0 + 1:k0 + kn], in_=sap(Wbuf, 0, [[1, G], [2 * G, kn - 1]]), func=AF.Copy),
                   reads=[B["Wbuf"]], partial=[B["Xb"]])
            OP("dve", lambda e, kn=kn: e.tensor_copy(out=Zc[:], in_=sap(Wbuf, (kn - 1) * 2 * G, [[G, 2], [1, G]])), reads=[B["Wbuf"]], writes=[B["Zc"]])
        rotmod[0] = 8
        B["hs"].start_gen()
        for gh in range(2):
            slotM, rbM = ring_load([(0, [[1, 4096]], bass.AP(s5scr[0], gh * 4096, [[G * 128, 128], [1, 4096]]))], extra=S5W.rd())
            slotQ, rbQ = ring_load([(0, [[1, 4096]], bass.AP(s5scr[3], gh * 4096, [[G * 128, 128], [1, 4096]]))], extra=S5W.rd())
            for g4 in range(0, 32, 4):
                b = next_bank()
                pbuf[b].start_gen()
                for gg in range(4):
                    gl = g4 + gg
                    g = gh * 32 + gl
                    OP("pe", lambda e, b=b, g=g, gl=gl, gg=gg, slotM=slotM: e.matmul(PB[b][0:KT, gg * 128:(gg + 1) * 128], lhsT=U[:, g, 0:KT],
                                                                                   rhs=ring[:, slotM, gl * 128:(gl + 1) * 128], start=True, stop=False),
                       reads=[rbM, B["U"]], partial=[pbuf[b]])
                    OP("pe", lambda e, b=b, g=g, gl=gl, gg=gg, slotQ=slotQ: e.matmul(PB[b][0:KT, gg * 128:(gg + 1) * 128], lhsT=Xb[:, g, 0:KT],
                                                                                   rhs=ring[:, slotQ, gl * 128:(gl + 1) * 128], start=False, stop=True),
                       reads=[rbQ, B["Xb"]], partial=[pbuf[b]])
                gbase = gh * 32 + g4
                outap = sap(hs, 16 * gbase, [[16, 4], [D, 8], [1, 16]], parts=KT)
                inap = PB[b].ap(0, [[128, 4], [16, 8], [1, 16]], parts=KT)
                OP("act", lambda e, outap=outap, inap=inap: e.activation(out=outap, in_=inap, func=AF.Gelu_apprx_tanh), reads=[pbuf[b]], partial=[B["hs"]])
        transposes_to_fm(KT, None, xnT, B["xnT"])
        for q in range(2):
            slot, rb = ring_load([(0, [[512, 8], [1, 512]], wsrc(wglu_d, 0, D + 512 * q, 512))])
            B["sgb"].start_gen()
            for j in range(8):
                b = next_bank()
                pbuf[b].start_gen()
                for fc in range(FC):
                    OP("pe", lambda e, b=b, j=j, fc=fc, slot=slot: e.matmul(PB[b][0:KT, :], lhsT=sap(xnT, fc * 1024 + j, [[8, KT]]),
                                                                          rhs=ring[:, slot, fc * 512:(fc + 1) * 512], start=(fc == 0), stop=(fc == FC - 1)),
                       reads=[rb, B["xnT"]], partial=[pbuf[b]])
                OP("act", lambda e, b=b, j=j: e.activation(out=sgb[0:KT, j, :], in_=PB[b][0:KT, :], func=AF.Sigmoid), reads=[pbuf[b]], partial=[B["sgb"]])
            slot, rb = ring_load([(0, [[512, 8], [1, 512]], wsrc(wglu_d, 0, 512 * q, 512))])
            for j in range(8):
                b = next_bank()
                pbuf[b].start_gen()
                for fc in range(FC):
                    OP("pe", lambda e, b=b, j=j, fc=fc, slot=slot: e.matmul(PB[b][0:KT, :], lhsT=sap(xnT, fc * 1024 + j, [[8, KT]]),
                                                                          rhs=ring[:, slot, fc * 512:(fc + 1) * 512], start=(fc == 0), stop=(fc == FC - 1)),
                       reads=[rb, B["xnT"]], partial=[pbuf[b]])
                o = OP("dve", lambda e, b=b, j=j: e.tensor_tensor(out=sgb[0:KT, j, :], in0=PB[b][0:KT, :], in1=sgb[0:KT, j, :], op=ALU.mult),
                       reads=[pbuf[b], B["sgb"]])
                B["sgb"].also_wrote(o)
                o = OP("dve", lambda e, j=j, q=q: e.tensor_tensor(out=h[0:KT, j, 512 * q:512 * q + 512], in0=h[0:KT, j, 512 * q:512 * q + 512],
                                                                  in1=sgb[0:KT, j, :], op=ALU.add), reads=[B["sgb"], B["h"]], extra=B["h"].wr())
                B["h"].also_wrote(o)
                if q == 1:
                    square_row(KT, j)
        rotmod[0] = 4
        rot[0] = 0

    def ffn_layer(KT, li, gi):
        NT = 8 * KT
        rmsnorm_to_hs(KT, perm=False)
        transposes_to_fm(KT, gi, xnT, B["xnT"])
        cgs = colgroups(NT)
        rotmod[0] = 8
        B["act"].start_gen()
        B["woutb"].start_gen()
        for m in range(MF):
            slot, rb = ring_load([
                (0, [[256, 8], [1, 128]], bass.AP(win_d, li * D * 2 * DFF + m * 128, [[2 * DFF, 128], [128 * 2 * DFF, 8], [1, 128]])),
                (128, [[256, 8], [1, 128]], bass.AP(win_d, li * D * 2 * DFF + DFF + m * 128, [[2 * DFF, 128], [128 * 2 * DFF, 8], [1, 128]]))])
            DMA("pool", lambda e, m=m: e.dma_start(out=woutb[:, m, :], in_=bass.AP(wout_d, li * DFF * D + m * 128 * D, [[D, 128], [1, D]])),
                "wout", partial=[B["woutb"]])
            for ci, (c0, cn) in enumerate(cgs):
                bg = next_bank()
                pbuf[bg].start_gen()
                for fc in range(FC):
                    OP("pe", lambda e, bg=bg, fc=fc, slot=slot, c0=c0, cn=cn: e.matmul(PB[bg][:, 0:cn], lhsT=ring[:, slot, fc * 256:fc * 256 + 128],
                                                                                     rhs=xnT[:, fc, c0:c0 + cn], start=(fc == 0), stop=(fc == FC - 1)),
                       reads=[rb, B["xnT"]], partial=[pbuf[bg]])
                bu = next_bank()
                pbuf[bu].start_gen()
                for fc in range(FC):
                    OP("pe", lambda e, bu=bu, fc=fc, slot=slot, c0=c0, cn=cn: e.matmul(PB[bu][:, 0:cn], lhsT=ring[:, slot, fc * 256 + 128:fc * 256 + 256],
                                                                                     rhs=xnT[:, fc, c0:c0 + cn], start=(fc == 0), stop=(fc == FC - 1)),
                       reads=[rb, B["xnT"]], partial=[pbuf[bu]])
                sl = (m * 2 + ci) % 2
                sB = B[f"silu{sl}"]
                OP("act", lambda e, bg=bg, sl=sl, cn=cn: e.activation(out=silu_t[:, sl, 0:cn], in_=PB[bg][:, 0:cn], func=AF.Silu), reads=[pbuf[bg]], writes=[sB])
                OP("dve", lambda e, bu=bu, sl=sl, m=m, c0=c0, cn=cn: e.tensor_tensor(out=act[:, m, c0:c0 + cn], in0=PB[bu][:, 0:cn], in1=silu_t[:, sl, 0:cn],
                                                                                   op=ALU.mult), reads=[pbuf[bu], sB], partial=[B["act"]])
        for j in range(8):
            for half in range(2):
                b = next_bank()
                pbuf[b].start_gen()
                for m in range(MF):
                    OP("pe", lambda e, b=b, j=j, m=m, half=half: e.matmul(PB[b][0:KT, :], lhsT=sap(act, m * 1024 + j, [[8, KT]]),
                                                                        rhs=woutb[:, m, 512 * half:512 * half + 512], start=(m == 0), stop=(m == MF - 1)),
                       reads=[B["act"], B["woutb"]], partial=[pbuf[b]])
                o = OP("dve", lambda e, b=b, j=j, half=half: e.tensor_tensor(out=h[0:KT, j, 512 * half:512 * half + 512], in0=PB[b][0:KT, :],
                                                                            in1=h[0:KT, j, 512 * half:512 * half + 512], op=ALU.add),
                       reads=[pbuf[b], B["h"]], extra=B["h"].wr())
                B["h"].also_wrote(o)
                if half == 1:
                    square_row(KT, j)

    _ffn_inner = ffn_layer

    def ffn_layer(KT, li, gi):
        try:
            _ffn_inner(KT, li, gi)
        finally:
            rotmod[0] = 4
            rot[0] = 0

    KVB = Buf()

    def kv_proj(KT, tok0):
        NT = 8 * KT
        rmsnorm_to_hs(KT, perm=False)
        if KT == 128:
            transposes_to_fm(KT, 2, xnT, B["xnT"], 3, xnT2, B["xnT2"])
        else:
            transposes_to_fm(KT, 2, xnT, B["xnT"])
        cgs = colgroups(NT)
        rotmod[0] = 8
        B["Kst"].start_gen()
        cnt = 0
        for eh in range(2):
            slot, rb = ring_load([(0, [[512, 8], [1, 512]], wsrc(wkv_d, 0, 512 * eh, 512))])
            for e4 in range(4):
                ech = eh * 4 + e4
                for (c0, cn) in cgs:
                    b = next_bank()
                    pbuf[b].start_gen()
                    for fc in range(FC):
                        OP("pe", lambda e, b=b, fc=fc, slot=slot, e4=e4, c0=c0, cn=cn: e.matmul(PB[b][:, 0:cn], lhsT=ring[:, slot, fc * 512 + e4 * 128:fc * 512 + e4 * 128 + 128],
                                                                                              rhs=xnT[:, fc, c0:c0 + cn], start=(fc == 0), stop=(fc == FC - 1)),
                           reads=[rb, B["xnT"]], partial=[pbuf[b]])
                    if cnt % 2 == 0:
                        OP("act", lambda e, b=b, ech=ech, c0=c0, cn=cn: e.activation(out=Kst[:, ech, c0:c0 + cn], in_=PB[b][:, 0:cn], func=AF.Copy),
                           reads=[pbuf[b]], partial=[B["Kst"]])
                    else:
                        OP("dve", lambda e, b=b, ech=ech, c0=c0, cn=cn: e.tensor_copy(out=Kst[:, ech, c0:c0 + cn], in_=PB[b][:, 0:cn]),
                           reads=[pbuf[b]], partial=[B["Kst"]])
                    cnt += 1
        DMA("sp", lambda e: e.dma_start(out=bass.AP(KT_all, tok0, [[L, 128], [128 * L, 8], [1, NT]]), in_=Kst[:, :, 0:NT]), "kvst",
            reads=[B["Kst"]], partial=[KVB])
        B["Vst"].start_gen()
        tbs = [(t0, min(128, NT - t0)) for t0 in range(0, NT, 128)]
        for vh in range(2):
            slot, rb = ring_load([(0, [[512, 8], [1, 512]], wsrc(wkv_d, 0, D + 512 * vh, 512))])
            for ti, (t0, tn) in enumerate(tbs):
                b = next_bank()
                pbuf[b].start_gen()
                for fc in range(FC):
                    OP("pe", lambda e, b=b, fc=fc, slot=slot, t0=t0, tn=tn: e.matmul(PB[b][0:tn, :], lhsT=xnT[:, fc, t0:t0 + tn],
                                                                                   rhs=ring[:, slot, fc * 512:(fc + 1) * 512], start=(fc == 0), stop=(fc == FC - 1)),
                       reads=[rb, B["xnT"]], partial=[pbuf[b]])
                if cnt % 2 == 0:
                    OP("act", lambda e, b=b, ti=ti, tn=tn, vh=vh: e.activation(out=Vst[0:tn, ti, 512 * vh:512 * vh + 512], in_=PB[b][0:tn, :], func=AF.Copy),
                       reads=[pbuf[b]], partial=[B["Vst"]])
                else:
                    OP("dve", lambda e, b=b, ti=ti, tn=tn, vh=vh: e.tensor_copy(out=Vst[0:tn, ti, 512 * vh:512 * vh + 512], in_=PB[b][0:tn, :]),
                       reads=[pbuf[b]], partial=[B["Vst"]])
                cnt += 1
        if NT >= 128:
            DMA("sp", lambda e: e.dma_start(out=bass.AP(V_all, tok0 * D, [[D, 128], [128 * D, NT // 128], [1, D]]), in_=Vst[:, 0:NT // 128, :]), "kvst",
                reads=[B["Vst"]], partial=[KVB])
        else:
            DMA("sp", lambda e: e.dma_start(out=bass.AP(V_all, tok0 * D, [[D, NT], [1, D]]), in_=Vst[0:NT, 0, :]), "kvst",
                reads=[B["Vst"]], partial=[KVB])
        rotmod[0] = 4
        rot[0] = 0

    kvslotB = [Buf(), Buf()]
    tmpB = {n: [Buf(), Buf()] for n in ("E", "SP", "ARG", "LM", "W")}
    ABANK = [4, 5]
    OBANK = [6, 7]
    chain_ctr = [0]
    kvl_ctr = [0]

    def attention(a, tok0):
        KT = 128
        NT = 1024
        B["QT"].start_gen()
        cnt = 0
        for eh in range(2):
            slot, rb = ring_load([(0, [[512, 8], [1, 512]], wsrc(wq_d, 0, 512 * eh, 512))])
            for e4 in range(4):
                ech = eh * 4 + e4
                for (c0, cn) in colgroups(NT):
                    b = next_bank()
                    pbuf[b].start_gen()
                    for fc in range(FC):
                        OP("pe", lambda e, b=b, fc=fc, slot=slot, e4=e4, c0=c0, cn=cn: e.matmul(PB[b][:, 0:cn], lhsT=ring[:, slot, fc * 512 + e4 * 128:fc * 512 + e4 * 128 + 128],
                                                                                              rhs=xnT2[:, fc, c0:c0 + cn], start=(fc == 0), stop=(fc == FC - 1)),
                           reads=[rb, B["xnT2"]], partial=[pbuf[b]])
                    if cnt % 2 == 0:
                        OP("act", lambda e, b=b, ech=ech, c0=c0, cn=cn: e.activation(out=QT[:, ech, c0:c0 + cn], in_=PB[b][:, 0:cn], func=AF.Copy),
                           reads=[pbuf[b]], partial=[B["QT"]])
                    else:
                        OP("dve", lambda e, b=b, ech=ech, c0=c0, cn=cn: e.tensor_copy(out=QT[:, ech, c0:c0 + cn], in_=PB[b][:, 0:cn]),
                           reads=[pbuf[b]], partial=[B["QT"]])
                    cnt += 1
        alias_fence([t_ for n_ in tmpB for t_ in tmpB[n_]], [B["xnT2"]])
        nk = NMETA + 1024 * (a + 1)
        nrb = 8 * (a + 1)
        B["oT"].start_gen()

        nfull = 8 * (a + 1)

        def load_kv(hp):
            ks = hp % 2
            kb_ = kvslotB[ks]
            kb_.start_gen()
            DMA("sp", lambda e, hp=hp, ks=ks: e.dma_start(out=KTp[:, ks, 0:nk], in_=bass.AP(KT_all, hp * 128 * L, [[L, 128], [1, nk]])), f"kvl{ks}",
                partial=[kb_], extra=KVB.rd())
            DMA("sp", lambda e, hp=hp, ks=ks: e.dma_start(out=sap(Vp, ks * 33 * 128, [[128, nfull], [1, 128]]),
                                                         in_=bass.AP(V_all, hp * 128, [[D, 128], [128 * D, nfull], [1, 128]])), f"kvl{ks}",
                partial=[kb_], extra=KVB.rd())
            DMA("sp", lambda e, hp=hp, ks=ks: e.dma_start(out=Vp[0:NMETA, ks, nfull * 128:nfull * 128 + 128],
                                                         in_=bass.AP(V_all, 128 * nfull * D + hp * 128, [[D, NMETA], [1, 128]])), f"kvl{ks}",
                partial=[kb_], extra=KVB.rd())

        steps = []
        for hp in range(8):
            for qs in range(2):
                nbq = 4 * (2 * a + qs + 1)
                blocks = [("tail", nbq, 4)] + [("full", b_, b_ - (nbq - 4)) for b_ in range(nbq - 1, -1, -1)]
                for bi, (kind, rbi, dd) in enumerate(blocks):
                    steps.append(dict(hp=hp, qs=qs, bi=bi, nblk=len(blocks), kind=kind, rbi=rbi, dd=dd, ci=hp * 2 + qs,
                                      first_of_hp=(qs == 0 and bi == 0)))
        NS = len(steps)
        APAIR = 4
        a_read = [None]

        def geom(st):
            kp = NMETA if st["kind"] == "tail" else 128
            kc0 = 128 * st["rbi"]
            vcol = 128 * st["rbi"]
            diag = st["dd"] >= 0
            c0 = max(0, 128 * st["dd"] - 16) if diag else 0
            return kp, kc0, vcol, diag, c0

        def maskap(st, kp):
            if st["dd"] == 0:
                return 128, sap(dmask, 512, [[0, 2], [1, 128]], parts=kp)
            wid = 16 if st["kind"] == "tail" else 128
            return wid, sap(dmask, 0, [[0, 2], [1, wid]], parts=kp)

        def wv(t, sl, kp, c0, wid=None):
            wid = 512 - c0 if wid is None else wid
            return sap(t, sl * 1024 + c0, [[512, 2], [1, wid]], parts=kp)

        def zpair(sl, kp, c0):
            return bass.AP(PBt, 1024 * sl + c0, [[4096, kp], [512, 2], [1, 512 - c0]])

        def apair(kp, c0):
            return bass.AP(PBt, 512 * APAIR + c0, [[4096, kp], [512, 2], [1, 512 - c0]])

        def st1(T):
            st = steps[T]
            sl = T % 2
            kp, kc0, vcol, diag, c0 = geom(st)
            ks, hp, q0 = st["hp"] % 2, st["hp"], 512 * st["qs"]
            for hh in range(2):
                zb = 2 * sl + hh
                pbuf[zb].start_gen()
                prs = slice(64 * hh, 64 * hh + 64)
                OP("pe", lambda e, zb=zb, kp=kp, kc0=kc0, ks=ks, prs=prs, hp=hp, q0=q0, c0=c0: e.matmul(
                    PB[zb][0:kp, c0:512], lhsT=KTp[prs, ks, kc0:kc0 + kp], rhs=QT[prs, hp, q0 + c0:q0 + 512], start=True, stop=True),
                   reads=[kvslotB[ks], B["QT"]], partial=[pbuf[zb]])

        def st2(T):
            st = steps[T]
            sl = T % 2
            kp, kc0, vcol, diag, c0 = geom(st)
            OP("act", lambda e, sl=sl, kp=kp, c0=c0: e.activation(out=wv(Et, sl, kp, c0), in_=zpair(sl, kp, c0), func=AF.Exp, scale=-0.125),
               reads=[pbuf[2 * sl], pbuf[2 * sl + 1]], writes=[tmpB["E"][sl]])
            OP("act", lambda e, sl=sl, kp=kp, c0=c0: e.activation(out=wv(SPt, sl, kp, c0), in_=wv(Et, sl, kp, c0), func=AF.Ln, bias=1.0),
               reads=[tmpB["E"][sl]], writes=[tmpB["SP"][sl]])

        def st3(T):
            st = steps[T]
            sl = T % 2
            kp, kc0, vcol, diag, c0 = geom(st)
            OP("dve", lambda e, sl=sl, kp=kp, c0=c0: e.scalar_tensor_tensor(out=wv(LMt, sl, kp, c0), in0=zpair(sl, kp, c0), scalar=-0.125,
                                                                          in1=wv(SPt, sl, kp, c0), op0=ALU.mult, op1=ALU.subtract),
               reads=[pbuf[2 * sl], pbuf[2 * sl + 1], tmpB["SP"][sl]], writes=[tmpB["LM"][sl]])
            if diag:
                mw, mk = maskap(st, kp)
                OP("dve", lambda e, sl=sl, kp=kp, c0=c0, mw=mw, mk=mk: e.tensor_tensor(out=wv(LMt, sl, kp, c0, mw), in0=wv(LMt, sl, kp, c0, mw),
                                                                                     in1=mk, op=ALU.mult),
                   reads=[tmpB["LM"][sl], CONST], writes=[tmpB["LM"][sl]])

        def st4(T):
            st = steps[T]
            sl = T % 2
            kp, kc0, vcol, diag, c0 = geom(st)
            first = st["bi"] == 0
            for hh in range(2):
                ab = APAIR + hh
                if first:
                    pbuf[ab].start_gen()
                extra = [a_read[0]] if a_read[0] is not None else []
                OP("pe", lambda e, ab=ab, kp=kp, sl=sl, hh=hh, first=first, c0=c0: e.matmul(
                    PB[ab][:, c0:512], lhsT=TU[0:kp, :], rhs=sap(LMt, sl * 1024 + 512 * hh + c0, [[1, 512 - c0]], parts=kp),
                    start=first, stop=False, skip_group_check=True),
                   reads=[tmpB["LM"][sl], CONST], partial=[pbuf[ab]], extra=extra)

        def st5(T):
            st = steps[T]
            sl = T % 2
            kp, kc0, vcol, diag, c0 = geom(st)
            a_read[0] = OP("dve", lambda e, sl=sl, kp=kp, c0=c0: e.tensor_tensor(out=wv(ARGt, sl, kp, c0), in0=apair(kp, c0),
                                                                               in1=wv(SPt, sl, kp, c0), op=ALU.subtract),
                           reads=[pbuf[APAIR], pbuf[APAIR + 1], tmpB["SP"][sl]], writes=[tmpB["ARG"][sl]])
            pbuf[APAIR].did_read(a_read[0])
            pbuf[APAIR + 1].did_read(a_read[0])

        def st6(T):
            st = steps[T]
            sl = T % 2
            kp, kc0, vcol, diag, c0 = geom(st)
            if st["bi"] == st["nblk"] - 1:
                return
            for hh in range(2):
                ab = APAIR + hh
                OP("pe", lambda e, ab=ab, kp=kp, sl=sl, hh=hh, c0=c0: e.matmul(
                    PB[ab][:, c0:512], lhsT=TL[0:kp, :], rhs=sap(LMt, sl * 1024 + 512 * hh + c0, [[1, 512 - c0]], parts=kp),
                    start=False, stop=False, skip_group_check=True),
                   reads=[tmpB["LM"][sl], CONST], partial=[pbuf[ab]], extra=[a_read[0]])

        def st7(T):
            st = steps[T]
            sl = T % 2
            kp, kc0, vcol, diag, c0 = geom(st)
            OP("act", lambda e, sl=sl, kp=kp, c0=c0: e.activation(out=wv(Wtt, sl, kp, c0), in_=wv(ARGt, sl, kp, c0), func=AF.Exp),
               reads=[tmpB["ARG"][sl]], writes=[tmpB["W"][sl]])
            if diag:
                mw, mk = maskap(st, kp)
                OP("dve", lambda e, sl=sl, kp=kp, c0=c0, mw=mw, mk=mk: e.tensor_tensor(out=wv(Wtt, sl, kp, c0, mw), in0=wv(Wtt, sl, kp, c0, mw),
                                                                                     in1=mk, op=ALU.mult),
                   reads=[tmpB["W"][sl], CONST], writes=[tmpB["W"][sl]])

        def st8(T):
            st = steps[T]
            sl = T % 2
            kp, kc0, vcol, diag, c0 = geom(st)
            ks, hp, q0 = st["hp"] % 2, st["hp"], 512 * st["qs"]
            ob = 6 + st["ci"] % 2
            first = st["bi"] == 0
            last = st["bi"] == st["nblk"] - 1
            if first:
                pbuf[ob].start_gen()
            for hh in range(2):
                OP("pe", lambda e, ob=ob, kp=kp, sl=sl, ks=ks, vcol=vcol, hh=hh, first=first, last=last, c0=c0: e.matmul(
                    PB[ob][64 * hh:64 * hh + 64, c0:512], lhsT=Vp[0:kp, ks, vcol + 64 * hh:vcol + 64 * hh + 64],
                    rhs=sap(Wtt, sl * 1024 + 512 * hh + c0, [[1, 512 - c0]], parts=kp), start=first, stop=last, skip_group_check=True),
                   reads=[tmpB["W"][sl], kvslotB[ks]], partial=[pbuf[ob]])
            if last:
                OP("act", lambda e, ob=ob, hp=hp, q0=q0: e.activation(out=oT[:, hp, q0:q0 + 512], in_=PB[ob][:, :], func=AF.Copy),
                   reads=[pbuf[ob]], partial=[B["oT"]])

        load_kv(0)
        load_kv(1)
        hp_first_T = {}
        for T, st in enumerate(steps):
            if st["first_of_hp"]:
                hp_first_T[st["hp"]] = T
        st1(0)
        for T in range(NS + 3):
            if 0 <= T - 1 < NS:
                st4(T - 1)
                st5(T - 1)
            if T + 1 < NS:
                st1(T + 1)
            if 0 <= T - 1 < NS:
                st6(T - 1)
            if T < NS:
                st2(T)
            if 0 <= T - 2 < NS:
                st7(T - 2)
            if T < NS:
                st3(T)
            if 0 <= T - 3 < NS:
                st8(T - 3)
            for hp in range(1, 7):
                if hp_first_T[hp] + 4 == T:
                    load_kv(hp + 1)
        for half in range(2):
            slot, rb = ring_load([(0, [[512, 8], [1, 512]], wsrc(wo_d, 0, 512 * half, 512))])
            for j in range(8):
                b = next_bank()
                pbuf[b].start_gen()
                for hp in range(8):
                    OP("pe", lambda e, b=b, j=j, hp=hp, slot=slot: e.matmul(PB[b][:, :], lhsT=sap(oT, hp * 1024 + j, [[8, 128]]),
                                                                          rhs=ring[:, slot, hp * 512:(hp + 1) * 512], start=(hp == 0), stop=(hp == 7)),
                       reads=[rb, B["oT"]], partial=[pbuf[b]])
                o = OP("dve", lambda e, b=b, j=j, half=half: e.tensor_tensor(out=h[:, j, 512 * half:512 * half + 512], in0=PB[b][:, :],
                                                                            in1=h[:, j, 512 * half:512 * half + 512], op=ALU.add),
                       reads=[pbuf[b], B["h"]], extra=B["h"].wr())
                B["h"].also_wrote(o)
                if half == 1:
                    square_row(128, j)

    def final_norm_store(a):
        KT = 128
        if presq[0]:
            presq[0] = False
        else:
            B["ssq"].start_gen()
            B["hs"].start_gen()
            for j in range(8):
                OP("act", lambda e, j=j: e.activation(out=hs[0:KT, j * D:(j + 1) * D], in_=h[0:KT, j, :], func=AF.Square, accum_out=ssq[0:KT, j:j + 1]),
                   reads=[B["h"]], partial=[B["ssq"], B["hs"]])
        OP("act", lambda e: e.activation(out=rstd[0:KT, :], in_=ssq[0:KT, :], func=AF.Ln, scale=1.0 / D, bias=EPS), reads=[B["ssq"]], writes=[B["rstd"]])
        OP("act", lambda e: e.activation(out=rstd[0:KT, :], in_=rstd[0:KT, :], func=AF.Exp, scale=-0.5), reads=[B["rstd"]], writes=[B["rstd"]])
        B["outs"].start_gen()
        for j in range(8):
            OP("dve", lambda e, j=j: e.scalar_tensor_tensor(out=outs[:, j, :], in0=h[:, j, :], scalar=rstd[:, j:j + 1], in1=gfin[:], op0=ALU.mult, op1=ALU.mult),
               reads=[B["h"], B["rstd"], CONST], partial=[B["outs"]])
        return DMA("sp", lambda e: e.dma_start(out=bass.AP(out_d, a * 1024 * D, [[8 * D, 128], [1, 8 * D]]), in_=outs[:].rearrange("p j f -> p (j f)")),
                   "outst", reads=[B["outs"]])

    out_dmas = []
    setup_done_deps = []
    for b_ in setup_bufs:
        setup_done_deps += b_.wr()
    setup_done_deps += S5W.rd()
    for nm in ("h", "hs", "xnT", "U", "Wbuf", "Xb", "act", "woutb"):
        B[nm].w = {}
        B[nm].r = {}
    fence = S.add("sp", lambda e: e.nop(), setup_done_deps)
    for nm in ("h", "hs", "xnT", "U", "Wbuf", "Xb", "act", "woutb", "Kst", "Vst", "QT", "oT", "sgb"):
        B[nm].r["sp"] = fence
    for kb in kvslotB:
        kb.r["sp"] = fence
    for n_ in tmpB:
        for t_ in tmpB[n_]:
            t_.r["sp"] = fence

    P_S5 = [B["U"], B["Wbuf"], B["Xb"], B["sgb"]]
    P_FFN = [B["act"], B["woutb"], B["silu0"], B["silu1"]]
    P_KV = [B["Kst"], B["Vst"], B["xnT2"]]
    P_ATT = [B["QT"], B["oT"]] + kvslotB + [t_ for n_ in tmpB for t_ in tmpB[n_]]
    last_phase = [None]
    tiles = [("meta", 0, 2, 0)] + [("real", a, 128, NMETA + 1024 * a) for a in range(4)]
    stages = ["s5", "ffn0", "kv", "attn", "ffn1", "all"]
    lvl = stages.index(upto)
    for (kind, a, KT, tok0) in tiles:
        if kind == "meta":
            DMA("sp", lambda e: e.dma_start(out=h[0:2].rearrange("p j f -> p (j f)"), in_=bass.AP(meta_d, 0, [[8 * D, 2], [1, 8 * D]])), "xld",
                writes=[B["h"]])
        else:
            DMA("sp", lambda e, a=a: e.dma_start(out=h[:].rearrange("p j f -> p (j f)"), in_=bass.AP(x_d, a * 1024 * D, [[8 * D, 128], [1, 8 * D]])), "xld",
                writes=[B["h"]])
        if last_phase[0] is not None:
            alias_fence(P_S5, last_phase[0])
        s5_layer(KT)
        last_phase[0] = P_S5
        if lvl >= 1:
            alias_fence(P_FFN, last_phase[0])
            ffn_layer(KT, 0, 1)
            last_phase[0] = P_FFN
        if lvl >= 2:
            alias_fence(P_KV, last_phase[0])
            kv_proj(KT, tok0)
            last_phase[0] = P_KV
        if kind == "real":
            if lvl >= 3:
                alias_fence(P_ATT, last_phase[0])
                attention(a, tok0)
                last_phase[0] = P_ATT
            if lvl >= 4:
                alias_fence(P_FFN, last_phase[0])
                ffn_layer(KT, 1, 4)
                last_phase[0] = P_FFN
            if lvl >= 5:
                alias_fence([B["outs"]], last_phase[0])
                out_dmas.append(final_norm_store(a))
                last_phase[0] = list(last_phase[0]) + [B["outs"]]
        if lvl < 5:
            dbg_dump_h(KT, tok0)
            if dbg_d is not None:
                out_dmas.append(S.ops["sp"][-1])
    S.add("sp", lambda e: e.nop(), out_dmas)
    S.emit()
    return nc


_NC_CACHE = {}


def kernel(**inputs):
    if "nc" not in _NC_CACHE:
        _NC_CACHE["nc"] = build_nc()
    nc = _NC_CACHE["nc"]
    f = lambda a: np.ascontiguousarray(np.asarray(a, dtype=np.float32))
    shared = {
        "meta_tokens": f(inputs["meta_tokens"]),
        "norm_mix": f(inputs["norm_mix"]),
        "norm_ffn": f(inputs["norm_ffn"]),
        "s5_a_re": f(inputs["s5_a_re"]).reshape(G, 64),
        "s5_a_im": f(inputs["s5_a_im"]).reshape(G, 64),
        "s5_log_dt": f(inputs["s5_log_dt"]).reshape(1, G),
        "s5_b_re": f(inputs["s5_b_re"]).reshape(G, 64, 16),
        "s5_b_im": f(inputs["s5_b_im"]).reshape(G, 64, 16),
        "s5_c_re": f(inputs["s5_c_re"]).reshape(G * 16, 64),
        "s5_c_im": f(inputs["s5_c_im"]).reshape(G * 16, 64),
        "s5_d": f(inputs["s5_d"]).reshape(1, D),
        "s5_w_glu": f(inputs["s5_w_glu"]).reshape(D, 2 * D),
        "norm_kv": f(inputs["norm_kv"]).reshape(1, D),
        "w_kv": f(inputs["w_kv"]),
        "w_q": f(inputs["w_q"]).reshape(D, D),
        "w_o": f(inputs["w_o"]).reshape(D, D),
        "w_ffn_in": f(inputs["w_ffn_in"]),
        "w_ffn_out": f(inputs["w_ffn_out"]),
        "norm_final": f(inputs["norm_final"]).reshape(1, D),
    }
    x = f(inputs["x"])
    in_maps = [dict(shared, x=x[b]) for b in range(8)]
    res = run_bass_kernel_spmd(nc, in_maps, core_ids=list(range(8)))
    return np.stack([np.asarray(r["out"], dtype=np.float32) for r in res.results], axis=0)
```
